# Optimizing a Trainium2 kernel written in Bass

```python
import jax, jax.numpy as jnp
from jax import lax
import numpy as np

D_MODEL = 1024
BATCH = 2
SEQ = 8192
DEPTH = 1
DEC_BATCH = 8
DEC_SEQ = 8192
PAST_LEN = 128

GRID_W = 64
POOL_WINDOWS = (2, 4, 8, 16)
POOL_GROUPS = 4
D_POOL = D_MODEL // 2
POOL_GROUP_C = D_POOL // POOL_GROUPS
POOL_OUT_C = D_MODEL // POOL_GROUPS
N_HEADS = D_MODEL // 128
D_ATT_QK = D_MODEL // 2
HEAD_DIM_QK = D_ATT_QK // N_HEADS
HEAD_DIM_V = D_MODEL // N_HEADS
NA_WIN_H = 8
NA_WIN_W = 16
D_IN = D_POOL + 2 * D_ATT_QK + D_MODEL + 2 * D_MODEL
D_FF = ((8 * D_MODEL + 3 * 256 - 1) // (3 * 256)) * 256
EPS = 1e-6

kernel_name = "hybrid_pool_natten_encoder"


def rmsnorm(x, g):
    x32 = x.astype(jnp.float32)
    y = x32 * lax.rsqrt(jnp.mean(x32 * x32, axis=-1, keepdims=True) + EPS)
    return (y * g.astype(jnp.float32)).astype(x.dtype)


def multiscale_pool(u, w_grp, scale):
    B, S, _ = u.shape
    u32 = u.astype(jnp.float32)
    cs = jnp.concatenate([jnp.zeros((B, 1, D_POOL), jnp.float32), jnp.cumsum(u32, axis=1)], axis=1)
    t = jnp.arange(S)
    outs = []
    for g, w in enumerate(POOL_WINDOWS):
        lo = jnp.clip(t - w // 2, 0, S)
        hi = jnp.clip(t + w // 2, 0, S)
        sl = slice(g * POOL_GROUP_C, (g + 1) * POOL_GROUP_C)
        csg = cs[:, :, sl]
        win_sum = jnp.take(csg, hi, axis=1) - jnp.take(csg, lo, axis=1)
        cnt = (hi - lo).astype(jnp.float32)[None, :, None]
        outs.append(win_sum / cnt - u32[:, :, sl])
    pooled = jnp.stack(outs, axis=2).astype(u.dtype)
    y = jnp.einsum('bsgc,gcd->bsgd', pooled, w_grp)
    return y.reshape(B, S, D_MODEL) * scale


def neighbourhood_attention(q, k, v, rpb):
    B, S = q.shape[0], q.shape[1]
    rows = S // GRID_W
    kh = min(NA_WIN_H, rows)
    kw = NA_WIN_W
    qg = q.reshape(B, rows, GRID_W, N_HEADS, HEAD_DIM_QK)
    kg = k.reshape(B, rows, GRID_W, N_HEADS, HEAD_DIM_QK)
    vg = v.reshape(B, rows, GRID_W, N_HEADS, HEAD_DIM_V)
    cols = jnp.arange(GRID_W)
    col_start = jnp.clip(cols - kw // 2, 0, GRID_W - kw)
    col_idx = col_start[:, None] + jnp.arange(kw)[None, :]
    dc = col_idx - cols[:, None]
    row_start = jnp.clip(jnp.arange(rows) - kh // 2, 0, rows - kh)
    rpb_c = rpb[:, :, dc + (NA_WIN_W - 1)]
    scale = HEAD_DIM_QK ** -0.5

    def one_row(r):
        rs = row_start[r]
        q_r = lax.dynamic_index_in_dim(qg, r, axis=1, keepdims=False)
        k_blk = lax.dynamic_slice_in_dim(kg, rs, kh, axis=1)
        v_blk = lax.dynamic_slice_in_dim(vg, rs, kh, axis=1)
        k_nb = k_blk[:, :, col_idx]
        v_nb = v_blk[:, :, col_idx]
        s = jnp.einsum('bchd,bkcjhd->bhckj', q_r, k_nb).astype(jnp.float32) * scale
        dr = rs + jnp.arange(kh) - r
        bias = rpb_c[:, dr + (NA_WIN_H - 1)]
        s = s + jnp.transpose(bias, (0, 2, 1, 3))[None].astype(jnp.float32)
        p = jax.nn.softmax(s.reshape(B, N_HEADS, GRID_W, kh * kw), axis=-1)
        p = p.reshape(B, N_HEADS, GRID_W, kh, kw).astype(v.dtype)
        return jnp.einsum('bhckj,bkcjhe->bche', p, v_nb)

    out = lax.map(one_row, jnp.arange(rows))
    return jnp.transpose(out, (1, 0, 2, 3, 4)).reshape(B, S, N_HEADS * HEAD_DIM_V)


def encoder_layer(x, norm_mix_pre, w_in, w_pool_grp, pool_scale, attn_rpb, w_out,
                  norm_mix_post, norm_ffn_pre, w_gate_up, w_down, norm_ffn_post):
    B, S, _ = x.shape
    h = rmsnorm(x, norm_mix_pre)
    z = h @ w_in
    o1 = D_POOL
    o2 = o1 + D_ATT_QK
    o3 = o2 + D_ATT_QK
    o4 = o3 + D_MODEL
    o5 = o4 + D_MODEL
    u_pool = z[..., :o1]
    q = z[..., o1:o2].reshape(B, S, N_HEADS, HEAD_DIM_QK)
    k = z[..., o2:o3].reshape(B, S, N_HEADS, HEAD_DIM_QK)
    v = z[..., o3:o4].reshape(B, S, N_HEADS, HEAD_DIM_V)
    g_pool = jax.nn.sigmoid(z[..., o4:o5])
    g_attn = jax.nn.sigmoid(z[..., o5:])
    a = multiscale_pool(u_pool, w_pool_grp, pool_scale)
    b = neighbourhood_attention(q, k, v, attn_rpb)
    m = g_pool * a + g_attn * b
    x = x + rmsnorm(m @ w_out, norm_mix_post)
    h = rmsnorm(x, norm_ffn_pre)
    gu = h @ w_gate_up
    f = (jax.nn.silu(gu[..., :D_FF]) * gu[..., D_FF:]) @ w_down
    return x + rmsnorm(f, norm_ffn_post)


def trunk(x, norm_mix_pre, w_in, w_pool_grp, pool_scale, attn_rpb, w_out,
          norm_mix_post, norm_ffn_pre, w_gate_up, w_down, norm_ffn_post):
    for l in range(DEPTH):
        x = encoder_layer(x, norm_mix_pre[l], w_in[l], w_pool_grp[l], pool_scale[l], attn_rpb[l],
                          w_out[l], norm_mix_post[l], norm_ffn_pre[l], w_gate_up[l], w_down[l],
                          norm_ffn_post[l])
    return x


def setup_inputs(seed: int = 0) -> dict:
    key = jax.random.key(seed)
    ks = jax.random.split(key, 14)
    f32 = jnp.float32

    def nrm(k, shape, s):
        return jax.random.normal(k, shape, f32) * s

    def gain(k):
        return 1.0 + nrm(k, (DEPTH, D_MODEL), 0.05)

    return {
        "x_prompt": nrm(ks[0], (BATCH, SEQ, D_MODEL), 1.0),
        "x_sample": nrm(ks[1], (DEC_BATCH, DEC_SEQ, D_MODEL), 1.0),
        "norm_mix_pre": gain(ks[2]),
        "w_in": nrm(ks[3], (DEPTH, D_MODEL, D_IN), D_MODEL ** -0.5),
        "w_pool_grp": nrm(ks[4], (DEPTH, POOL_GROUPS, POOL_GROUP_C, POOL_OUT_C), POOL_GROUP_C ** -0.5),
        "pool_scale": 1.0 + nrm(ks[5], (DEPTH, D_MODEL), 0.1),
        "attn_rpb": nrm(ks[6], (DEPTH, N_HEADS, 2 * NA_WIN_H - 1, 2 * NA_WIN_W - 1), 0.5),
        "w_out": nrm(ks[7], (DEPTH, D_MODEL, D_MODEL), D_MODEL ** -0.5),
        "norm_mix_post": gain(ks[8]),
        "norm_ffn_pre": gain(ks[9]),
        "w_gate_up": nrm(ks[10], (DEPTH, D_MODEL, 2 * D_FF), D_MODEL ** -0.5),
        "w_down": nrm(ks[11], (DEPTH, D_FF, D_MODEL), D_FF ** -0.5),
        "norm_ffn_post": gain(ks[12]),
    }


def reference(x_prompt, x_sample, norm_mix_pre, w_in, w_pool_grp, pool_scale, attn_rpb, w_out,
              norm_mix_post, norm_ffn_pre, w_gate_up, w_down, norm_ffn_post):
    y_prompt = trunk(x_prompt, norm_mix_pre, w_in, w_pool_grp, pool_scale, attn_rpb, w_out,
                     norm_mix_post, norm_ffn_pre, w_gate_up, w_down, norm_ffn_post)
    y_sample = trunk(x_sample, norm_mix_pre, w_in, w_pool_grp, pool_scale, attn_rpb, w_out,
                     norm_mix_post, norm_ffn_pre, w_gate_up, w_down, norm_ffn_post)
    return (y_prompt, y_sample)
```

```python
import numpy as np
import concourse.bass as bass
import concourse.mybir as mybir
from concourse.bass_utils import run_bass_kernel_spmd

F32 = mybir.dt.float32
BF16 = mybir.dt.bfloat16
AF = mybir.ActivationFunctionType
ALU = mybir.AluOpType

P = 128
D = 1024
DFF = 2816
NH = 8
NCORE = 8
TOK_CORE = 10240
NBLK_FULL = 20
SEQ = 8192
MMW = 22
EBW = MMW * 64
EPS = 1e-6
NEG = -30000.0
NSLOT = 4


class Op:
    __slots__ = ("eng", "fns", "deps", "dma_key", "sig", "sem", "val", "idx")


class Sched:
    ENGS = ("pe", "act", "dve", "pool", "sp")

    def __init__(self):
        self.ops = []
        self.last_writer = {}
        self.readers = {}
        self.dma_count = {}

    def op(self, eng, fns, reads=(), writes=(), dma_key=None):
        o = Op()
        o.idx = len(self.ops)
        o.eng = eng
        o.fns = fns if isinstance(fns, (list, tuple)) else [fns]
        deps = set()
        for r in reads:
            w = self.last_writer.get(r)
            if w is not None:
                deps.add(w)
        for r in writes:
            w = self.last_writer.get(r)
            if w is not None:
                deps.add(w)
            deps.update(self.readers.get(r, ()))
        for r in reads:
            self.readers.setdefault(r, []).append(o.idx)
        for r in writes:
            self.last_writer[r] = o.idx
            self.readers[r] = []
        o.deps = deps
        o.dma_key = dma_key
        o.sig = False
        o.sem = None
        o.val = 0
        self.ops.append(o)
        return o

    def finalize(self, nc, sems):
        ops = self.ops
        for o in ops:
            for d in o.deps:
                od = ops[d]
                if od.eng == "pe" and o.eng == "pe" and od.dma_key is None:
                    continue
                od.sig = True
        cnt = {e: 0 for e in self.ENGS}
        dcnt = {}
        for o in ops:
            if o.dma_key is not None:
                dcnt[o.dma_key] = dcnt.get(o.dma_key, 0) + 1
                o.sem = sems["dma:" + o.dma_key]
                o.val = 16 * dcnt[o.dma_key]
                o.sig = True
            elif o.sig:
                cnt[o.eng] += 1
                o.sem = sems["eng:" + o.eng]
                o.val = cnt[o.eng]
        for o in ops:
            if o.dma_key in ("pro0", "pro1"):
                o.val = 16 * dcnt[o.dma_key]
        self.max_counts = dict(cnt)
        self.dma_counts = dcnt

    def emit(self, eng_name, eng):
        ops = self.ops
        waited = {}
        for o in ops:
            if o.eng != eng_name:
                continue
            need = {}
            for d in o.deps:
                od = ops[d]
                if od.eng == "pe" and o.eng == "pe" and od.dma_key is None:
                    continue
                k = id(od.sem)
                if k not in need or need[k][1] < od.val:
                    need[k] = (od.sem, od.val)
            for k, (sem, val) in need.items():
                if waited.get(k, 0) >= val:
                    continue
                eng.wait_ge(sem, val)
                waited[k] = val
            last = None
            for fn in o.fns:
                last = fn(eng)
            if o.sig and last is not None:
                last.then_inc(o.sem, 16 if o.dma_key is not None else 1)


def build(nblk, debug=False):
    nc = bass.Bass("TRN2", target_bir_lowering=False)
    S = Sched()
    ntok_in = 512 * nblk + 512

    def din(name, shape, dt=F32):
        return nc.dram_tensor(name, list(shape), dt, kind="ExternalInput").ap()

    xin = din("xin", [ntok_in, D])
    w_in = din("w_in", [D, 4608])
    w_gu = din("w_gu", [D, 2 * DFF])
    w_d = din("w_d", [DFF, D])
    w_out = din("w_out", [D, D])
    w_pool = din("w_pool", [4, 128, 256])
    gains = din("gains", [P, 4 * D])
    pscale = din("pscale", [P, 8])
    btab = din("btab", [P, NH * EBW])
    qmask = din("qmask", [nblk, P, 512])
    kone = din("kone", [P, 1024])
    edge = din("edge", [P, nblk * 64])
    hmask = din("hmask", [P, nblk * 2])
    ident = din("ident", [P, P])
    yout = nc.dram_tensor("yout", [512 * nblk, D], F32, kind="ExternalOutput").ap()
    dbg = {}
    if debug:
        for nm, shp in (("d_hT", [P, 8 * 1024]), ("d_KT", [P, 4 * 1024]), ("d_V", [P, 8 * 1024]),
                        ("d_QT", [P, 4 * 512]), ("d_pooled", [P, 4 * 512]), ("d_mT", [P, 8 * 512]),
                        ("d_h2T", [P, 8 * 512])):
            dbg[nm] = nc.dram_tensor(nm, shp, BF16, kind="ExternalOutput").ap()
        dbg["d_x1"] = nc.dram_tensor("d_x1", [P, 4 * 1024], F32, kind="ExternalOutput").ap()
        dbg["d_UT"] = nc.dram_tensor("d_UT", [P, 4 * 1024], F32, kind="ExternalOutput").ap()

    s_win = nc.dram_tensor("s_win", [9, P, 4096], BF16).ap()
    s_wout = nc.dram_tensor("s_wout", [2, P, 4096], BF16).ap()
    s_wgu = nc.dram_tensor("s_wgu", [11, P, 4096], BF16).ap()
    s_wd = nc.dram_tensor("s_wd", [6, P, 4096], BF16).ap()
    c_KT = nc.dram_tensor("c_KT", [P, 4, 512], BF16).ap()
    c_UT = nc.dram_tensor("c_UT", [P, 4, 512], F32).ap()
    c_V = nc.dram_tensor("c_V", [P, 4, 1024], BF16).ap()

    def sb(name, shape, dt):
        return nc.alloc_sbuf_tensor(name, list(shape), dt)

    wring = [sb(f"wr{i}", [P, 4096], BF16) for i in range(NSLOT)]
    wpool_sb = sb("wpool", [P, 4, 256], BF16)
    ebt = sb("ebt", [P, NH, EBW], BF16)
    gtab = sb("gtab", [P, 4, D], F32)
    pscale_sb = sb("pscale_sb", [P, 8], F32)
    pscale_h = sb("pscale_h", [P, 8], F32)
    hmask_sb = sb("hmask_sb", [P, nblk * 2], F32)
    edge_sb = sb("edge_sb", [P, nblk * 64], F32)
    ident_sb = sb("ident_sb", [P, P], BF16)
    ones_sb = sb("ones_sb", [P, P], BF16)
    kone_sb = sb("kone_sb", [P, 1024], BF16)
    qmask_sb = [sb(f"qmask{i}", [P, 512], BF16) for i in range(2)]
    xs = [sb(f"xs{i}", [P, D], F32) for i in range(2)]
    hb = [sb(f"hb{i}", [P, D], BF16) for i in range(2)]
    hT = sb("hT", [P, 8, 1024], BF16)
    arenaA = sb("arenaA", [P, 12288], BF16)
    UT = arenaA[:, 0:8192].bitcast(F32).rearrange("p (c t) -> p c t", c=4)
    KT = arenaA[:, 8192:12288].rearrange("p (c t) -> p c t", c=4)
    ACTT = arenaA[:, 0:22 * 512].rearrange("p (j t) -> p j t", j=22)
    arenaB = sb("arenaB", [P, 8192], BF16)
    V = arenaB[:, :].rearrange("p (t f) -> p t f", t=8)
    X1 = arenaB[:, :].bitcast(F32).rearrange("p (t f) -> p t f", t=4)
    QT = sb("QTz", [P, 8, 512], BF16)
    pooled = sb("pooled", [P, 4, 512], BF16)
    mT = sb("mT", [P, 8, 512], BF16)
    NPT = 6
    PT = [sb(f"PT{i}", [P, 512], BF16) for i in range(NPT)]
    tnh = [sb(f"tnh{i}", [P, 512], F32) for i in range(3)]
    t1 = sb("t1", [P, 512], F32)
    t1b = sb("t1b", [P, 512], F32)
    tmpS = [sb(f"tmpS{i}", [P, 512], F32) for i in range(2)]
    hb2 = [sb(f"hbx{i}", [P, D], BF16) for i in range(2)]
    t2 = sb("t2", [P, 512], F32)
    t3 = sb("t3", [P, 512], F32)
    rden = sb("rden", [P, 512], F32)
    pa = sb("pa", [P, 528], F32)
    pb = sb("pb", [P, 528], F32)
    stats = sb("stats", [P, 64], F32)
    negh = sb("negh", [P, 2], F32)
    ps = [nc.alloc_psum_tensor(f"ps{i}", [P, 512], F32) for i in range(8)]

    st_i = [0]

    def stat_col():
        st_i[0] = (st_i[0] + 1) % 64
        return st_i[0]

    mm_i = [0]

    def mm_bank():
        mm_i[0] = (mm_i[0] + 1) % 4
        return mm_i[0]

    acc_i = [0]

    def acc_pair():
        acc_i[0] = (acc_i[0] + 1) % 2
        return 4 + 2 * acc_i[0], 5 + 2 * acc_i[0]

    alt = [0]

    def evac_engine():
        alt[0] ^= 1
        return "act" if alt[0] else "dve"

    def copy_fn(engname, out, in_):
        if engname == "act":
            return lambda e: e.copy(out, in_)
        return lambda e: e.tensor_copy(out, in_)

    units = []
    for _b in range(nblk):
        for u in (0, 2, 3, 4, 1, 5, 7, 6, 8):
            units.append((s_win[u], ("scr", "win", u)))
        for u in range(2):
            units.append((s_wout[u], ("scr", "wout", u)))
        for u in range(11):
            units.append((s_wgu[u], ("scr", "wgu", u)))
        for _pss in range(2):
            for u in range(6):
                units.append((s_wd[u], ("scr", "wd", u)))
    wstate = {"issued": 0, "next": 0, "released": 0}

    def w_pump():
        while wstate["issued"] < len(units) and wstate["issued"] - NSLOT < wstate["released"]:
            j = wstate["issued"]
            slot = j % NSLOT
            src, key = units[j]
            ncol = 2048 if key == ("scr", "wd", 5) else 4096
            S.op("sp", (lambda e, slot=slot, src=src, ncol=ncol: e.dma_start(out=wring[slot][:, 0:ncol], in_=src[:, 0:ncol])),
                 reads=[key], writes=[("wr", slot)], dma_key=f"wr{slot}")
            wstate["issued"] += 1

    def w_acquire(hold_prev=0):
        j = wstate["next"]
        wstate["next"] += 1
        wstate["released"] = max(wstate["released"], j - hold_prev)
        w_pump()
        assert wstate["issued"] > j, (j, wstate)
        slot = j % NSLOT
        return slot, wring[slot]

    def pro_dma(out, in_, writes, grp="pro1"):
        S.op("pool", (lambda e: e.dma_start(out=out, in_=in_)), reads=[], writes=writes, dma_key=grp)

    pro_dma(wpool_sb[:, :, :], w_pool.rearrange("g c n -> c g n"), [("wpool",)], grp="pro0")
    pro_dma(ident_sb[:, :], ident, [("ident",)], grp="pro0")
    pro_dma(kone_sb[:, :], kone, [("kone",)], grp="pro0")
    w_in_v = w_in.rearrange("(k p) n -> p k n", p=P)
    for u in range(9):
        pro_dma(s_win[u].rearrange("p (k n) -> p k n", k=8), w_in_v[:, :, u * 512:(u + 1) * 512],
                [("scr", "win", u)], grp="pro0")
    w_out_v = w_out.rearrange("(k p) n -> p k n", p=P)
    for u in range(2):
        pro_dma(s_wout[u].rearrange("p (k n) -> p k n", k=8), w_out_v[:, :, u * 512:(u + 1) * 512],
                [("scr", "wout", u)])
    w_gu_v = w_gu.rearrange("(k p) n -> p k n", p=P)
    for u in range(11):
        dst = s_wgu[u].rearrange("p (a k n) -> p a k n", a=2, k=8)
        pro_dma(dst[:, 0], w_gu_v[:, :, u * 256:(u + 1) * 256], [("scr", "wgu", u, 0)])
        pro_dma(dst[:, 1], w_gu_v[:, :, DFF + u * 256:DFF + (u + 1) * 256], [("scr", "wgu", u, 1)])
    w_d_v = w_d.rearrange("(j p) n -> p j n", p=P)
    for u in range(6):
        nj = 4 if u < 5 else 2
        dst = s_wd[u].rearrange("p (j n) -> p j n", j=4)
        pro_dma(dst[:, 0:nj], w_d_v[:, 4 * u:4 * u + nj, :], [("scr", "wd", u)])
    for u in range(11):
        units_key = ("scr", "wgu", u)
        S.last_writer[units_key] = S.last_writer[("scr", "wgu", u, 1)]

    def sp_load(out, in_, writes, key):
        S.op("sp", (lambda e: e.dma_start(out=out, in_=in_)), reads=[], writes=writes, dma_key=key)

    sp_load(gtab[:, :, :], gains.rearrange("p (a n) -> p a n", a=4), [("gtab",)], "c0")
    sp_load(pscale_sb[:, :], pscale, [("pscale",)], "c1")
    sp_load(hmask_sb[:, :], hmask, [("hmask",)], "c2")
    sp_load(edge_sb[:, :], edge, [("edge",)], "c3")
    S.op("dve", (lambda e: e.memset(ones_sb[:, :], 1.0)), writes=[("ones",)])
    S.op("pool", (lambda e: e.memset(QT[:, :, :], 0.0)), writes=[("QT", h) for h in range(NH)])
    S.op("dve", (lambda e: e.memset(negh[:, :], -0.5)), writes=[("negh",)])
    S.op("dve", (lambda e: e.tensor_scalar(pscale_h[:, :], pscale_sb[:, :], 0.5, None, ALU.mult)),
         reads=[("pscale",)], writes=[("pscale_h",)])
    for h in range(NH):
        for half in range(2):
            i = (2 * h + half) % 2
            c0 = h * EBW + half * 704
            sp_load(xs[i][:, 0:704], btab[:, c0:c0 + 704], [("xs", i)], f"xs{i}")
            S.op("act", (lambda e, i=i, h=h, half=half: e.activation(
                out=ebt[:, h, half * 704:(half + 1) * 704], in_=xs[i][:, 0:704], func=AF.Copy, scale=8.0)),
                reads=[("xs", i)], writes=[("ebt", h, half)])
    EBT_R = [("ebt", h, half) for h in range(NH) for half in range(2)]

    def rstd_from(ss_ap, out_ap, reads, wkey):
        S.op("pool", (lambda e: e.tensor_scalar(out_ap, ss_ap, 1.0 / D, EPS, ALU.mult, ALU.add)),
             reads=reads, writes=[wkey])
        S.op("pool", (lambda e: e.tensor_tensor(out_ap, out_ap, negh[:, 0:1], ALU.pow)),
             reads=[wkey, ("negh",)], writes=[wkey])

    jk_i = [0]

    junkA = pa[:, :].bitcast(BF16)[:, 0:D]
    junkB = pb[:, :].bitcast(BF16)[:, 0:D]

    def junk_buf():
        jk_i[0] ^= 1
        return (junkA, ("pa",)) if jk_i[0] else (junkB, ("pb",))

    def norm_front(src_ap, src_reads, gidx, hbt, hbkey):
        c = stat_col()
        jb, jkey = junk_buf()
        S.op("act", (lambda e: e.activation(out=jb[:, :], in_=src_ap, func=AF.Square,
                                            accum_out=stats[:, c:c + 1])),
             reads=src_reads, writes=[("st", c), jkey])
        c2 = stat_col()
        rstd_from(stats[:, c:c + 1], stats[:, c2:c2 + 1], [("st", c)], ("st", c2))
        S.op("dve", (lambda e: e.scalar_tensor_tensor(hbt[:, :], src_ap, stats[:, c2:c2 + 1],
                                                       gtab[:, gidx, :], ALU.mult, ALU.mult)),
             reads=list(src_reads) + [("st", c2), ("gtab",)], writes=[hbkey])

    def norm_back(dstT, dst_keys, tcol0, hbt, hbkey):
        bank = mm_bank()
        psb = ps[bank][:, :].bitcast(BF16).rearrange("p (k t) -> p k t", k=8)
        S.op("pe", [(lambda e, k=k: e.transpose(psb[:, k, :], hbt[:, k * P:(k + 1) * P], ident_sb[:, :]))
                    for k in range(8)],
             reads=[hbkey, ("ident",)], writes=[("ps", bank)])
        en = evac_engine()
        S.op(en, copy_fn(en, dstT[:, :, tcol0:tcol0 + P], psb),
             reads=[("ps", bank)], writes=dst_keys)

    def norm_to_T(src_ap, src_reads, gidx, dstT, dst_key_fn, tcol0, hbt, hbkey):
        norm_front(src_ap, src_reads, gidx, hbt, hbkey)
        norm_back(dstT, dst_key_fn(), tcol0, hbt, hbkey)

    xs_i = [0]

    p1_slot = {}

    def phase1_front(bb, tt):
        rr = 512 * bb
        i = xs_i[0] = (xs_i[0] + 1) % 2
        p1_slot[(bb, tt)] = i
        sp_load(xs[i][:, :], xin[rr + tt * P:rr + (tt + 1) * P, :], [("xs", i)], f"xs{i}")
        norm_front(xs[i][:, :], [("xs", i)], 0, hb[i], ("hb", i))

    def phase1_back(bb, tt):
        i = p1_slot[(bb, tt)]
        norm_back(hT, [("hT", tt)], tt * P, hb[i], ("hb", i))

    def phase1_tile(bb, tt):
        phase1_front(bb, tt)
        phase1_back(bb, tt)

    for tt in range(8):
        phase1_tile(0, tt)

    for b in range(nblk):
        r0 = 512 * b
        hT_all = [("hT", t) for t in range(8)]
        hT_half = [[("hT", t) for t in range(4)], [("hT", t) for t in range(4, 8)]]
        hT_mid = [("hT", t) for t in range(2, 6)]

        first_A = [True]

        def tokA(write):
            return ([], [("tokA",)]) if write else ([("tokA",)], [])

        first_B = [True]

        def fm_proj(slot_t, col0, rhs_lo, nN, hreads):
            bank = mm_bank()
            wv = slot_t[:, :].rearrange("p (k n) -> p k n", k=8)
            return bank, [(lambda e, k=k: e.matmul(ps[bank][:, :], wv[:, k, col0:col0 + P],
                                                   hT[:, k, rhs_lo:rhs_lo + 512],
                                                   start=(k == 0), stop=(k == 7))) for k in range(8)]

        carry = b >= 1
        hh_list = (1,) if carry else (0, 1)
        if carry:
            S.op("sp", (lambda e: e.dma_start(out=UT[:, :, 0:512], in_=c_UT)),
                 reads=[("car", "UT")], writes=[("tokA",)] + [("UT", c, 0) for c in range(4)], dma_key="cl0")
            S.op("sp", (lambda e: e.dma_start(out=KT[:, :, 0:512], in_=c_KT)),
                 reads=[("car", "KT"), ("tokA",)], writes=[("KT", c, 0) for c in range(4)], dma_key="cl1")
            S.op("sp", (lambda e: e.dma_start(out=V[:, 0:4, :], in_=c_V)),
                 reads=[("car", "V")], writes=[("tokB",)] + [("V", t, cb) for t in range(4) for cb in range(2)],
                 dma_key="cl2")
            first_A[0] = False
            first_B[0] = False
        slot, wt = w_acquire()
        for c in range(4):
            for hh in hh_list:
                bank, fns = fm_proj(wt, c * P, hh * 512, 512, None)
                S.op("pe", fns, reads=hT_half[hh] + [("wr", slot)], writes=[("ps", bank)])
                en = evac_engine()
                rd, wr = tokA(first_A[0])
                first_A[0] = False
                S.op(en, copy_fn(en, UT[:, c, hh * 512:(hh + 1) * 512], ps[bank][:, :]),
                     reads=[("ps", bank)] + rd, writes=[("UT", c, hh)] + wr)
        if b + 1 < nblk:
            S.op("sp", (lambda e: e.dma_start(out=c_UT, in_=UT[:, :, 512:1024])),
                 reads=[("UT", c, 1) for c in range(4)] + [("tokA",)], writes=[("car", "UT")], dma_key="cs0")
        for g in range(4):
            w = (2, 4, 8, 16)[g]
            ukeys = [("UT", g, 0), ("UT", g, 1)]
            S.op("dve", (lambda e, g=g, b=b: e.tensor_scalar(UT[:, g, 248:256], UT[:, g, 248:256],
                                                            hmask_sb[:, 2 * b:2 * b + 1], None, ALU.mult)),
                 reads=ukeys + [("hmask",), ("tokA",)], writes=[("UT", g, 0)])
            S.op("dve", (lambda e, g=g, b=b: e.tensor_scalar(UT[:, g, 768:776], UT[:, g, 768:776],
                                                            hmask_sb[:, 2 * b + 1:2 * b + 2], None, ALU.mult)),
                 reads=ukeys + [("hmask",), ("tokA",)], writes=[("UT", g, 1)])
            S.op("dve", (lambda e, g=g: e.tensor_tensor(pa[:, 0:527], UT[:, g, 248:775], UT[:, g, 249:776], ALU.add)),
                 reads=ukeys + [("tokA",)], writes=[("pa",)])
            cur, curk, off = pa, ("pa",), 7
            if g >= 1:
                S.op("dve", (lambda e: e.tensor_tensor(pb[:, 0:525], pa[:, 0:525], pa[:, 2:527], ALU.add)),
                     reads=[("pa",)], writes=[("pb",)])
                cur, curk, off = pb, ("pb",), 6
            if g >= 2:
                S.op("dve", (lambda e: e.tensor_tensor(pa[:, 0:521], pb[:, 0:521], pb[:, 4:525], ALU.add)),
                     reads=[("pb",)], writes=[("pa",)])
                cur, curk, off = pa, ("pa",), 4
            if g >= 3:
                S.op("dve", (lambda e: e.tensor_tensor(pb[:, 0:513], pa[:, 0:513], pa[:, 8:521], ALU.add)),
                     reads=[("pa",)], writes=[("pb",)])
                cur, curk, off = pb, ("pb",), 0
            S.op("dve", (lambda e, cur=cur, off=off, w=w: e.tensor_scalar(
                t1[:, :], cur[:, off:off + 512], 1.0 / w, None, ALU.mult)),
                reads=[curk], writes=[("t1",)])
            e0 = b * 64 + g * 16
            S.op("dve", (lambda e, cur=cur, off=off, e0=e0: e.tensor_tensor(
                t1[:, 0:8], cur[:, off:off + 8], edge_sb[:, e0:e0 + 8], ALU.mult)),
                reads=[curk, ("edge",), ("t1",)], writes=[("t1",)])
            S.op("dve", (lambda e, cur=cur, off=off, e0=e0: e.tensor_tensor(
                t1[:, 504:512], cur[:, off + 504:off + 512], edge_sb[:, e0 + 8:e0 + 16], ALU.mult)),
                reads=[curk, ("edge",), ("t1",)], writes=[("t1",)])
            S.op("dve", (lambda e, g=g: e.tensor_tensor(pooled[:, g, :], t1[:, :], UT[:, g, 256:768], ALU.subtract)),
                 reads=[("t1",)] + ukeys + [("tokA",)], writes=[("pooled", g)])

        slot, wt = w_acquire()
        for c in range(4):
            for hh in hh_list:
                bank, fns = fm_proj(wt, c * P, hh * 512, 512, None)
                S.op("pe", fns, reads=hT_half[hh] + [("wr", slot)], writes=[("ps", bank)])
                en = "act"
                S.op(en, copy_fn(en, KT[:, c, hh * 512:(hh + 1) * 512], ps[bank][:, :]),
                     reads=[("ps", bank), ("tokA",)], writes=[("KT", c, hh)])
        if b + 1 < nblk:
            S.op("sp", (lambda e: e.dma_start(out=c_KT, in_=KT[:, :, 512:1024])),
                 reads=[("KT", c, 1) for c in range(4)] + [("tokA",)], writes=[("car", "KT")], dma_key="cs1")
        for cb in range(2):
            slot, wt = w_acquire()
            wv = wt[:, :].rearrange("p (k n) -> p k n", k=8)
            for tt in (range(4, 8) if carry else range(8)):
                bank = mm_bank()
                S.op("pe", [(lambda e, k=k, tt=tt, bank=bank, wv=wv: e.matmul(
                    ps[bank][:, :], hT[:, k, tt * P:(tt + 1) * P], wv[:, k, :],
                    start=(k == 0), stop=(k == 7))) for k in range(8)],
                    reads=[("hT", tt), ("wr", slot)], writes=[("ps", bank)])
                en = evac_engine()
                if first_B[0]:
                    rd, wr = [], [("tokB",)]
                    first_B[0] = False
                else:
                    rd, wr = [("tokB",)], []
                S.op(en, copy_fn(en, V[:, tt, cb * 512:(cb + 1) * 512], ps[bank][:, :]),
                     reads=[("ps", bank)] + rd, writes=[("V", tt, cb)] + wr)
        if b + 1 < nblk:
            S.op("sp", (lambda e: e.dma_start(out=c_V, in_=V[:, 4:8, :])),
                 reads=[("V", t, cb) for t in range(4, 8) for cb in range(2)] + [("tokB",)], writes=[("car", "V")],
                 dma_key="cs2")
        slot, wt = w_acquire()
        for c in range(4):
            bank, fns = fm_proj(wt, c * P, 256, 512, None)
            S.op("pe", fns, reads=hT_mid + [("wr", slot)], writes=[("ps", bank)])
            for hp in range(2):
                en = evac_engine()
                S.op(en, copy_fn(en, QT[hp * 64:(hp + 1) * 64, 2 * c + hp, :], ps[bank][hp * 64:(hp + 1) * 64, :]),
                     reads=[("ps", bank)], writes=[("QT", 2 * c + hp)])
        qi = b % 2
        S.op("pool", (lambda e, qi=qi, b=b: e.dma_start(out=qmask_sb[qi][:, :], in_=qmask[b])),
             reads=[], writes=[("qmask", qi)], dma_key=f"qm{qi}")

        if debug and b == 0:
            def dump(name, src, reads):
                S.op("sp", (lambda e: e.dma_start(out=dbg[name], in_=src)), reads=reads, writes=[("dbg", name)],
                     dma_key="dbg")
            dump("d_hT", hT[:, :, :].rearrange("p k t -> p (k t)"), hT_all)
            dump("d_KT", arenaA[:, 8192:12288], [("KT", c, hh) for c in range(4) for hh in range(2)])
            dump("d_UT", arenaA[:, 0:8192].bitcast(F32), [("UT", c, hh) for c in range(4) for hh in range(2)] +
                 [("pooled", g) for g in range(4)])
            dump("d_V", arenaB[:, :], [("V", t, cb) for t in range(8) for cb in range(2)])
            pass
            dump("d_pooled", pooled[:, :, :].rearrange("p c t -> p (c t)"), [("pooled", g) for g in range(4)])

        e_i = [0]
        p_i = [0]
        gate_slots = {}
        head_acc = {}
        pt_of = {}
        LA = 4

        def head_pre(h):
            if h % 4 == 0:
                gate_slots["gp"] = w_acquire()
                gate_slots["ga"] = w_acquire(hold_prev=1)
            col = (h % 4) * P
            t1x = t1 if h % 2 == 0 else t1b
            t1k = ("t1", h % 2)
            gslot, gwt = gate_slots["gp"]
            bank, fns = fm_proj(gwt, col, 256, 512, None)
            S.op("pe", fns, reads=hT_mid + [("wr", gslot)], writes=[("ps", bank)])
            S.op("act", (lambda e, bank=bank: e.activation(out=tnh[2][:, :], in_=ps[bank][:, :], func=AF.Tanh, scale=0.5)),
                 reads=[("ps", bank)], writes=[("tnh", 2)])
            gslot, gwt = gate_slots["ga"]
            bank, fns = fm_proj(gwt, col, 256, 512, None)
            S.op("pe", fns, reads=hT_mid + [("wr", gslot)], writes=[("ps", bank)])
            ta = h % 2
            S.op("act", (lambda e, bank=bank, ta=ta: e.activation(out=tnh[ta][:, :], in_=ps[bank][:, :], func=AF.Tanh, scale=0.5)),
                 reads=[("ps", bank)], writes=[("tnh", ta)])
            g = h // 2
            abank = mm_bank()
            S.op("pe", (lambda e, abank=abank, g=g, h=h: e.matmul(
                ps[abank][:, :], wpool_sb[:, g, (h % 2) * P:(h % 2 + 1) * P], pooled[:, g, :], start=True, stop=True)),
                reads=[("wpool",), ("pooled", g)], writes=[("ps", abank)])
            S.op("act", (lambda e, abank=abank, h=h, t1x=t1x: e.activation(
                out=t1x[:, :], in_=ps[abank][:, :], func=AF.Copy, scale=pscale_h[:, h:h + 1])),
                reads=[("ps", abank), ("pscale_h",)], writes=[t1k])
            S.op("dve", (lambda e, t1x=t1x: e.scalar_tensor_tensor(t1x[:, :], tnh[2][:, :], 1.0, t1x[:, :], ALU.add, ALU.mult)),
                 reads=[t1k, ("tnh", 2)], writes=[t1k])

        def rec_S(h, p):
            ch = h // 2
            pl = (h % 2) * 64
            bank = mm_bank()
            c0 = (14 - 2 * p) * 64
            S.op("pe", [
                (lambda e, bank=bank, p=p, h=h, ch=ch: e.matmul(
                    ps[bank][:, :], KT[:, ch, p * P:(p + 1) * P],
                    QT[:, h, :], start=True, stop=False)),
                (lambda e, bank=bank, p=p, qi=qi: e.matmul(
                    ps[bank][:, :], kone_sb[:, p * P:(p + 1) * P],
                    qmask_sb[qi][:, :], start=False, stop=False)),
                (lambda e, bank=bank, c0=c0, h=h: e.matmul(
                    ps[bank][:, :], ident_sb[:, :], ebt[:, h, c0:c0 + 512], start=False, stop=True)),
            ], reads=[("KT", ch, p // 4), ("QT", h), ("kone",), ("qmask", qi), ("tokA",), ("ident",)] + EBT_R[2 * h:2 * h + 2],
                writes=[("ps", bank)])
            pi = p_i[0] = (p_i[0] + 1) % NPT
            S.op("act", (lambda e, bank=bank, pi=pi: e.activation(out=PT[pi][:, :], in_=ps[bank][:, :],
                                                                   func=AF.Exp, scale=0.125)),
                 reads=[("ps", bank)], writes=[("PT", pi)])
            pt_of[(h, p)] = pi

        def rec_PV(h, p):
            if p == 0:
                head_acc[h] = acc_pair()
            pvb, dnb = head_acc[h]
            pi = pt_of[(h, p)]
            S.op("pe", [
                (lambda e, pi=pi, p=p, h=h, pvb=pvb: e.matmul(ps[pvb][:, :], V[:, p, h * P:(h + 1) * P],
                                                              PT[pi][:, :], start=(p == 0), stop=(p == 7))),
                (lambda e, pi=pi, p=p, dnb=dnb: e.matmul(ps[dnb][:, :], ones_sb[:, :], PT[pi][:, :],
                                                         start=(p == 0), stop=(p == 7))),
            ], reads=[("PT", pi), ("V", p, h // 4), ("ones",), ("tokB",)],
                writes=[("ps", pvb), ("ps", dnb)])

        def head_post(h):
            pvb, dnb = head_acc[h]
            t1x = t1 if h % 2 == 0 else t1b
            t1k = ("t1", h % 2)
            ta = h % 2
            S.op("dve", (lambda e, dnb=dnb: e.reciprocal(rden[:, :], ps[dnb][:, :])),
                 reads=[("ps", dnb)], writes=[("rden",)])
            S.op("dve", (lambda e, pvb=pvb: e.scalar_tensor_tensor(t2[:, :], ps[pvb][:, :], 0.5, rden[:, :], ALU.mult, ALU.mult)),
                 reads=[("ps", pvb), ("rden",)], writes=[("t2",)])
            S.op("dve", (lambda e, ta=ta: e.scalar_tensor_tensor(t3[:, :], tnh[ta][:, :], 1.0, t2[:, :], ALU.add, ALU.mult)),
                 reads=[("t2",), ("tnh", ta)], writes=[("t3",)])
            S.op("pool", (lambda e, h=h, t1x=t1x: e.tensor_tensor(mT[:, h, :], t1x[:, :], t3[:, :], ALU.add)),
                 reads=[t1k, ("t3",)], writes=[("mT", h, t) for t in range(4)])

        tiles = [(h, p) for h in range(NH) for p in range(8)]
        for idx in range(len(tiles) + LA):
            if idx < len(tiles):
                h, p = tiles[idx]
                if p == 0:
                    head_pre(h)
                rec_S(h, p)
            j = idx - LA
            if j >= 0:
                h, p = tiles[j]
                rec_PV(h, p)
                if p == 7:
                    head_post(h)
        mT_all = [("mT", f, t) for f in range(8) for t in range(4)]
        if debug and b == 0:
            dump("d_mT", mT[:, :, :].rearrange("p c t -> p (c t)"), mT_all)

        S.op("sp", (lambda e, r0=r0: e.dma_start(
            out=X1[:, :, :], in_=xin[r0 + 256:r0 + 768, :].rearrange("(t p) f -> p t f", p=P))),
            reads=[], writes=[("tokB",)] + [("x1", t) for t in range(4)], dma_key="x1")
        wo = [w_acquire(), w_acquire(hold_prev=1)]
        p5 = {}

        def p5_A1(tb):
            banks = acc_pair()
            for cb in range(2):
                oslot, owt = wo[cb]
                wv = owt[:, :].rearrange("p (k n) -> p k n", k=8)
                S.op("pe", [(lambda e, f=f, cb=cb, tb=tb, wv=wv, bk=banks[cb]: e.matmul(
                    ps[bk][:, :], mT[:, f, tb * P:(tb + 1) * P], wv[:, f, :], start=(f == 0), stop=(f == 7)))
                    for f in range(8)],
                    reads=[("mT", f, tb) for f in range(8)] + [("wr", oslot)], writes=[("ps", banks[cb])])
            ca, cbb, cs = stat_col(), stat_col(), stat_col()
            for cb, cc in ((0, ca), (1, cbb)):
                jb, jkey = junk_buf()
                S.op("act", (lambda e, cb=cb, cc=cc, banks=banks, jb=jb: e.activation(
                    out=jb[:, 0:512], in_=ps[banks[cb]][:, :], func=AF.Square, accum_out=stats[:, cc:cc + 1])),
                    reads=[("ps", banks[cb])], writes=[("st", cc), jkey])
            S.op("dve", (lambda e, ca=ca, cbb=cbb, cs=cs: e.tensor_tensor(
                stats[:, cs:cs + 1], stats[:, ca:ca + 1], stats[:, cbb:cbb + 1], ALU.add)),
                reads=[("st", ca), ("st", cbb)], writes=[("st", cs)])
            p5[tb] = (banks, cs)

        def p5_A2(tb):
            banks, cs = p5[tb]
            cr = stat_col()
            rstd_from(stats[:, cs:cs + 1], stats[:, cr:cr + 1], [("st", cs)], ("st", cr))
            for cb in range(2):
                tmp = tmpS[cb]
                S.op("dve", (lambda e, cb=cb, cr=cr, banks=banks, tmp=tmp: e.scalar_tensor_tensor(
                    tmp[:, 0:512], ps[banks[cb]][:, :], stats[:, cr:cr + 1], gtab[:, 1, cb * 512:(cb + 1) * 512],
                    ALU.mult, ALU.mult)),
                    reads=[("ps", banks[cb]), ("st", cr), ("gtab",)], writes=[("tmpS", cb)])
                S.op("pool", (lambda e, cb=cb, tb=tb, tmp=tmp: e.tensor_tensor(
                    X1[:, tb, cb * 512:(cb + 1) * 512], X1[:, tb, cb * 512:(cb + 1) * 512], tmp[:, 0:512], ALU.add)),
                    reads=[("tmpS", cb), ("x1", tb), ("tokB",)], writes=[("x1", tb)])

        def p5_F(tb):
            norm_front(X1[:, tb, :], [("x1", tb), ("tokB",)], 2, hb2[tb % 2], ("hbx", tb % 2))

        def p5_Bk(tb):
            norm_back(mT, [("mT", f, tb) for f in range(8)], tb * P, hb2[tb % 2], ("hbx", tb % 2))

        p5_A1(0)
        p5_A1(1)
        p5_A2(0)
        p5_F(0)
        p5_A1(2)
        p5_A2(1)
        p5_Bk(0)
        p5_F(1)
        p5_A1(3)
        p5_A2(2)
        p5_Bk(1)
        p5_F(2)
        p5_A2(3)
        p5_Bk(2)
        p5_F(3)
        p5_Bk(3)
        h2_all = mT_all
        if debug and b == 0:
            dump("d_h2T", mT[:, :, :].rearrange("p c t -> p (c t)"), h2_all)
            dump("d_x1", arenaB[:, :].bitcast(F32), [("x1", t) for t in range(4)] + h2_all)

        first_act = [True]
        for u in range(11):
            slot, wt = w_acquire()
            wv = wt[:, :].rearrange("p (a k n) -> p a k n", a=2, k=8)
            for jj in range(2):
                j = 2 * u + jj
                gb = mm_bank()
                S.op("pe", [(lambda e, k=k, gb=gb, jj=jj, wv=wv: e.matmul(
                    ps[gb][:, :], wv[:, 0, k, jj * P:(jj + 1) * P], mT[:, k, :], start=(k == 0), stop=(k == 7)))
                    for k in range(8)],
                    reads=h2_all + [("wr", slot)], writes=[("ps", gb)])
                ub = mm_bank()
                S.op("pe", [(lambda e, k=k, ub=ub, jj=jj, wv=wv: e.matmul(
                    ps[ub][:, :], wv[:, 1, k, jj * P:(jj + 1) * P], mT[:, k, :], start=(k == 0), stop=(k == 7)))
                    for k in range(8)],
                    reads=h2_all + [("wr", slot)], writes=[("ps", ub)])
                ti = j % 2
                S.op("act", (lambda e, gb=gb, ti=ti: e.activation(out=tnh[ti][:, :], in_=ps[gb][:, :], func=AF.Tanh, scale=0.5)),
                     reads=[("ps", gb)], writes=[("tnh", ti)])
                tw = t1 if ti == 0 else t1b
                twk = ("t1", ti)
                S.op("dve", (lambda e, gb=gb, ti=ti, tw=tw: e.scalar_tensor_tensor(
                    tw[:, :], tnh[ti][:, :], 1.0, ps[gb][:, :], ALU.add, ALU.mult)),
                    reads=[("tnh", ti), ("ps", gb)], writes=[twk])
                if first_act[0]:
                    rd, wr = [], [("tokA",)]
                    first_act[0] = False
                else:
                    rd, wr = [("tokA",)], []
                S.op("dve", (lambda e, ub=ub, tw=tw, j=j: e.scalar_tensor_tensor(
                    ACTT[:, j, :], tw[:, :], 0.5, ps[ub][:, :], ALU.mult, ALU.mult)),
                    reads=[twk, ("ps", ub)] + rd, writes=[("actT", j)] + wr)
            if b + 1 < nblk:
                if 2 <= u <= 7:
                    phase1_back(b + 1, u)
                if 1 <= u <= 6:
                    phase1_front(b + 1, u + 1)
        act_all = [("actT", j) for j in range(22)]
        for pss in range(2):
            tbs = (2 * pss, 2 * pss + 1)
            bankmap = {tbs[0]: (4, 5), tbs[1]: (6, 7)}
            for u in range(6):
                slot, wt = w_acquire()
                nj = 4 if u < 5 else 2
                wv = wt[:, :].rearrange("p (j n) -> p j n", j=4)
                for tb in tbs:
                    for cb in range(2):
                        bk = bankmap[tb][cb]
                        S.op("pe", [(lambda e, jl=jl, u=u, tb=tb, cb=cb, bk=bk, wv=wv, nj=nj: e.matmul(
                            ps[bk][:, :], ACTT[:, 4 * u + jl, tb * P:(tb + 1) * P], wv[:, jl, cb * 512:(cb + 1) * 512],
                            start=(u == 0 and jl == 0), stop=(u == 5 and jl == nj - 1))) for jl in range(nj)],
                            reads=act_all + [("wr", slot), ("tokA",)], writes=[("ps", bk)])
            for tb in tbs:
                banks = bankmap[tb]
                ca, cbb, cs = stat_col(), stat_col(), stat_col()
                for cb, cc in ((0, ca), (1, cbb)):
                    jb, jkey = junk_buf()
                    S.op("act", (lambda e, cb=cb, cc=cc, banks=banks, jb=jb: e.activation(
                        out=jb[:, 0:512], in_=ps[banks[cb]][:, :], func=AF.Square, accum_out=stats[:, cc:cc + 1])),
                        reads=[("ps", banks[cb])], writes=[("st", cc), jkey])
                S.op("dve", (lambda e, ca=ca, cbb=cbb, cs=cs: e.tensor_tensor(
                    stats[:, cs:cs + 1], stats[:, ca:ca + 1], stats[:, cbb:cbb + 1], ALU.add)),
                    reads=[("st", ca), ("st", cbb)], writes=[("st", cs)])
                cr = stat_col()
                rstd_from(stats[:, cs:cs + 1], stats[:, cr:cr + 1], [("st", cs)], ("st", cr))
                for cb in range(2):
                    tmp = tmpS[cb]
                    S.op("dve", (lambda e, cb=cb, cr=cr, banks=banks, tmp=tmp: e.scalar_tensor_tensor(
                        tmp[:, 0:512], ps[banks[cb]][:, :], stats[:, cr:cr + 1], gtab[:, 3, cb * 512:(cb + 1) * 512],
                        ALU.mult, ALU.mult)),
                        reads=[("ps", banks[cb]), ("st", cr), ("gtab",)], writes=[("tmpS", cb)])
                    S.op("pool", (lambda e, cb=cb, tb=tb, tmp=tmp: e.tensor_tensor(
                        X1[:, tb, cb * 512:(cb + 1) * 512], X1[:, tb, cb * 512:(cb + 1) * 512], tmp[:, 0:512], ALU.add)),
                        reads=[("tmpS", cb), ("x1", tb), ("tokB",)], writes=[("x1", tb)])
                S.op("sp", (lambda e, tb=tb, b=b: e.dma_start(
                    out=yout[512 * b + tb * P:512 * b + (tb + 1) * P, :], in_=X1[:, tb, :])),
                    reads=[("x1", tb), ("tokB",)], writes=[("yout", b, tb)], dma_key="y")

    S.op("sp", [], reads=[], writes=[("yout", b, tb) for b in range(nblk) for tb in range(4)] +
         ([("dbg", k) for k in dbg] if debug else []))

    keys = set()
    for o in S.ops:
        if o.dma_key is not None:
            keys.add("dma:" + o.dma_key)
    for e in Sched.ENGS:
        keys.add("eng:" + e)
    sems = {k: nc.alloc_semaphore(k.replace(":", "_")) for k in sorted(keys)}
    S.finalize(nc, sems)
    with nc.Block() as block:
        @block.sync
        def _(eng):
            S.emit("sp", eng)

        @block.scalar
        def _(eng):
            S.emit("act", eng)

        @block.vector
        def _(eng):
            S.emit("dve", eng)

        @block.gpsimd
        def _(eng):
            S.emit("pool", eng)

        @block.tensor
        def _(eng):
            S.emit("pe", eng)
    return nc, S


def host_tables(attn_rpb, nblk, core, blk0=0):
    qmask = np.zeros((nblk, P, 512), np.float32)
    qmask[:, 0:16, :] = NEG
    edge = np.zeros((nblk, 4, 16), np.float32)
    hm = np.zeros((nblk, 2), np.float32)
    for b in range(nblk):
        G0 = core * TOK_CORE + 512 * (b + blk0)
        s0 = G0 % SEQ
        r0 = s0 // 64
        for i in range(8):
            r = r0 + i
            rs = min(max(r - 4, 0), 120)
            for u in range(16):
                kr = r0 - 4 + u
                if rs <= kr < rs + 8:
                    qmask[b, u, i * 64:(i + 1) * 64] = 0.0
        for g, w in enumerate((2, 4, 8, 16)):
            for j in range(16):
                t = j if j < 8 else 504 + (j - 8)
                s = s0 + t
                lo = max(s - w // 2, 0)
                hi = min(s + w // 2, SEQ)
                edge[b, g, j] = 1.0 / float(hi - lo)
        hm[b, 0] = 0.0 if s0 == 0 else 1.0
        hm[b, 1] = 0.0 if s0 + 512 == SEQ else 1.0
    edge_r = np.ascontiguousarray(np.broadcast_to(edge.reshape(1, -1), (P, nblk * 64)))
    hm_r = np.ascontiguousarray(np.broadcast_to(hm.reshape(1, -1), (P, nblk * 2)))
    return qmask, edge_r, hm_r


def bias_table(attn_rpb):
    rpb = np.asarray(attn_rpb, np.float32).reshape(NH, 15, 31)
    e = np.arange(2)[:, None, None, None]
    j = np.arange(64)[None, :, None, None]
    mm = np.arange(MMW)[None, None, :, None]
    c = np.arange(64)[None, None, None, :]
    dr = e + 10 - mm
    cs = np.clip(c - 8, 0, 48)
    ok = (np.abs(dr) <= 7) & (j >= cs) & (j < cs + 16)
    dri = np.clip(dr + 7, 0, 14)
    dci = np.clip(j - c + 15, 0, 30)
    dri_b = np.broadcast_to(dri, (2, 64, MMW, 64))
    dci_b = np.broadcast_to(dci, (2, 64, MMW, 64))
    ok_b = np.broadcast_to(ok, (2, 64, MMW, 64))
    out = np.empty((P, NH, EBW), np.float32)
    for h in range(NH):
        vals = rpb[h][dri_b, dci_b]
        tab = np.where(ok_b, vals, np.float32(NEG)).astype(np.float32)
        out[:, h, :] = tab.reshape(P, EBW)
    return out.reshape(P, NH * EBW)


def make_in_maps(inputs, nblk, blk0=0):
    xp = np.asarray(inputs["x_prompt"], np.float32).reshape(-1, D)
    xsm = np.asarray(inputs["x_sample"], np.float32).reshape(-1, D)
    xall = np.concatenate([xp, xsm], axis=0)
    ntot = xall.shape[0]
    gains = np.concatenate([np.asarray(inputs[k], np.float32).reshape(1, D) for k in
                            ("norm_mix_pre", "norm_mix_post", "norm_ffn_pre", "norm_ffn_post")], axis=1)
    gains_r = np.ascontiguousarray(np.broadcast_to(gains, (P, 4 * D)))
    pscale = np.ascontiguousarray(np.asarray(inputs["pool_scale"], np.float32).reshape(8, P).T)
    btab = bias_table(inputs["attn_rpb"])
    kone = np.zeros((P, 1024), np.float32)
    for u in range(16):
        kone[u, u * 64:(u + 1) * 64] = 1.0
    ident = np.eye(P, dtype=np.float32)
    shared = {
        "w_in": np.ascontiguousarray(np.asarray(inputs["w_in"], np.float32).reshape(D, 4608)),
        "w_gu": np.ascontiguousarray(np.asarray(inputs["w_gate_up"], np.float32).reshape(D, 2 * DFF)),
        "w_d": np.ascontiguousarray(np.asarray(inputs["w_down"], np.float32).reshape(DFF, D)),
        "w_out": np.ascontiguousarray(np.asarray(inputs["w_out"], np.float32).reshape(D, D)),
        "w_pool": np.ascontiguousarray(np.asarray(inputs["w_pool_grp"], np.float32).reshape(4, 128, 256)),
        "gains": gains_r, "pscale": pscale, "btab": btab, "kone": kone, "ident": ident,
    }
    in_maps = []
    ntok_in = 512 * nblk + 512
    for c in range(NCORE):
        lo = c * TOK_CORE + 512 * blk0 - 256
        hi = lo + ntok_in
        xin = np.zeros((ntok_in, D), np.float32)
        a, bnd = max(lo, 0), min(hi, ntot)
        xin[a - lo:bnd - lo] = xall[a:bnd]
        qm, edge_r, hm_r = host_tables(inputs["attn_rpb"], nblk, c, blk0)
        m = dict(shared)
        m.update({"xin": xin, "qmask": qm, "edge": edge_r, "hmask": hm_r})
        in_maps.append(m)
    return in_maps


_CACHE = {}


def kernel(x_prompt, x_sample, norm_mix_pre, w_in, w_pool_grp, pool_scale, attn_rpb, w_out,
           norm_mix_post, norm_ffn_pre, w_gate_up, w_down, norm_ffn_post):
    inputs = dict(x_prompt=x_prompt, x_sample=x_sample, norm_mix_pre=norm_mix_pre, w_in=w_in,
                  w_pool_grp=w_pool_grp, pool_scale=pool_scale, attn_rpb=attn_rpb, w_out=w_out,
                  norm_mix_post=norm_mix_post, norm_ffn_pre=norm_ffn_pre, w_gate_up=w_gate_up,
                  w_down=w_down, norm_ffn_post=norm_ffn_post)
    nblk = NBLK_FULL
    in_maps = make_in_maps(inputs, nblk)
    nc, _ = build(nblk)
    res = run_bass_kernel_spmd(nc, in_maps, core_ids=list(range(NCORE)))
    y = np.concatenate([np.asarray(r["yout"], np.float32) for r in res.results], axis=0)
    nprompt = np.asarray(x_prompt).shape[0] * np.asarray(x_prompt).shape[1]
    y_prompt = y[:nprompt].reshape(np.asarray(x_prompt).shape).astype(np.float32)
    y_sample = y[nprompt:].reshape(np.asarray(x_sample).shape).astype(np.float32)
    return (y_prompt, y_sample)
```

```python
import numpy as np
import concourse.bass as bass
import concourse.mybir as mybir
from concourse.bass_utils import run_bass_kernel_spmd

F32 = mybir.dt.float32
BF16 = mybir.dt.bfloat16
AF = mybir.ActivationFunctionType
ALU = mybir.AluOpType

P = 128
D = 1024
DFF = 2816
NH = 8
NCORE = 8
TOK_CORE = 10240
NBLK_FULL = 20
SEQ = 8192
MMW = 22
EBW = MMW * 64
EPS = 1e-6
NEG = -30000.0
NSLOT = 4


class Op:
    __slots__ = ("eng", "fns", "deps", "dma_key", "sig", "sem", "val", "idx")


class Sched:
    ENGS = ("pe", "act", "dve", "pool", "sp")

    def __init__(self):
        self.ops = []
        self.last_writer = {}
        self.readers = {}
        self.dma_count = {}

    def op(self, eng, fns, reads=(), writes=(), dma_key=None):
        o = Op()
        o.idx = len(self.ops)
        o.eng = eng
        o.fns = fns if isinstance(fns, (list, tuple)) else [fns]
        deps = set()
        for r in reads:
            w = self.last_writer.get(r)
            if w is not None:
                deps.add(w)
        for r in writes:
            w = self.last_writer.get(r)
            if w is not None:
                deps.add(w)
            deps.update(self.readers.get(r, ()))
        for r in reads:
            self.readers.setdefault(r, []).append(o.idx)
        for r in writes:
            self.last_writer[r] = o.idx
            self.readers[r] = []
        o.deps = deps
        o.dma_key = dma_key
        o.sig = False
        o.sem = None
        o.val = 0
        self.ops.append(o)
        return o

    def finalize(self, nc, sems):
        ops = self.ops
        for o in ops:
            for d in o.deps:
                od = ops[d]
                if od.eng == "pe" and o.eng == "pe" and od.dma_key is None:
                    continue
                od.sig = True
        cnt = {e: 0 for e in self.ENGS}
        dcnt = {}
        for o in ops:
            if o.dma_key is not None:
                dcnt[o.dma_key] = dcnt.get(o.dma_key, 0) + 1
                o.sem = sems["dma:" + o.dma_key]
                o.val = 16 * dcnt[o.dma_key]
                o.sig = True
            elif o.sig:
                cnt[o.eng] += 1
                o.sem = sems["eng:" + o.eng]
                o.val = cnt[o.eng]
        for o in ops:
            if o.dma_key in ("pro0", "pro1"):
                o.val = 16 * dcnt[o.dma_key]
        self.max_counts = dict(cnt)
        self.dma_counts = dcnt

    def emit(self, eng_name, eng):
        ops = self.ops
        waited = {}
        for o in ops:
            if o.eng != eng_name:
                continue
            need = {}
            for d in o.deps:
                od = ops[d]
                if od.eng == "pe" and o.eng == "pe" and od.dma_key is None:
                    continue
                k = id(od.sem)
                if k not in need or need[k][1] < od.val:
                    need[k] = (od.sem, od.val)
            for k, (sem, val) in need.items():
                if waited.get(k, 0) >= val:
                    continue
                eng.wait_ge(sem, val)
                waited[k] = val
            last = None
            for fn in o.fns:
                last = fn(eng)
            if o.sig and last is not None:
                last.then_inc(o.sem, 16 if o.dma_key is not None else 1)


def build(nblk, debug=False):
    nc = bass.Bass("TRN2", target_bir_lowering=False)
    S = Sched()
    ntok_in = 512 * nblk + 512

    def din(name, shape, dt=F32):
        return nc.dram_tensor(name, list(shape), dt, kind="ExternalInput").ap()

    xin = din("xin", [ntok_in, D])
    w_in = din("w_in", [D, 4608])
    w_gu = din("w_gu", [D, 2 * DFF])
    w_d = din("w_d", [DFF, D])
    w_out = din("w_out", [D, D])
    w_pool = din("w_pool", [4, 128, 256])
    gains = din("gains", [P, 4 * D])
    pscale = din("pscale", [P, 8])
    btab = din("btab", [P, NH * EBW])
    qmask = din("qmask", [nblk, P, 512])
    kone = din("kone", [P, 1024])
    edge = din("edge", [P, nblk * 64])
    hmask = din("hmask", [P, nblk * 2])
    ident = din("ident", [P, P])
    yout = nc.dram_tensor("yout", [512 * nblk, D], F32, kind="ExternalOutput").ap()
    dbg = {}
    if debug:
        for nm, shp in (("d_hT", [P, 8 * 1024]), ("d_KT", [P, 4 * 1024]), ("d_V", [P, 8 * 1024]),
                        ("d_QT", [P, 4 * 512]), ("d_pooled", [P, 4 * 512]), ("d_mT", [P, 8 * 512]),
                        ("d_h2T", [P, 8 * 512])):
            dbg[nm] = nc.dram_tensor(nm, shp, BF16, kind="ExternalOutput").ap()
        dbg["d_x1"] = nc.dram_tensor("d_x1", [P, 4 * 1024], F32, kind="ExternalOutput").ap()
        dbg["d_UT"] = nc.dram_tensor("d_UT", [P, 4 * 1024], F32, kind="ExternalOutput").ap()

    s_win = nc.dram_tensor("s_win", [9, P, 4096], BF16).ap()
    s_wout = nc.dram_tensor("s_wout", [2, P, 4096], BF16).ap()
    s_wgu = nc.dram_tensor("s_wgu", [11, P, 4096], BF16).ap()
    s_wd = nc.dram_tensor("s_wd", [6, P, 4096], BF16).ap()

    def sb(name, shape, dt):
        return nc.alloc_sbuf_tensor(name, list(shape), dt)

    wring = [sb(f"wr{i}", [P, 4096], BF16) for i in range(NSLOT)]
    wpool_sb = sb("wpool", [P, 4, 256], BF16)
    ebt = sb("ebt", [P, NH, EBW], BF16)
    gtab = sb("gtab", [P, 4, D], F32)
    pscale_sb = sb("pscale_sb", [P, 8], F32)
    pscale_h = sb("pscale_h", [P, 8], F32)
    hmask_sb = sb("hmask_sb", [P, nblk * 2], F32)
    edge_sb = sb("edge_sb", [P, nblk * 64], F32)
    ident_sb = sb("ident_sb", [P, P], BF16)
    ones_sb = sb("ones_sb", [P, P], BF16)
    kone_sb = sb("kone_sb", [P, 1024], BF16)
    qmask_sb = [sb(f"qmask{i}", [P, 512], BF16) for i in range(2)]
    xs = [sb(f"xs{i}", [P, D], F32) for i in range(2)]
    hb = [sb(f"hb{i}", [P, D], BF16) for i in range(2)]
    hT = sb("hT", [P, 8, 1024], BF16)
    arenaA = sb("arenaA", [P, 12288], BF16)
    UT = arenaA[:, 0:8192].bitcast(F32).rearrange("p (c t) -> p c t", c=4)
    KT = arenaA[:, 8192:12288].rearrange("p (c t) -> p c t", c=4)
    ACTT = arenaA[:, 0:22 * 512].rearrange("p (j t) -> p j t", j=22)
    arenaB = sb("arenaB", [P, 8192], BF16)
    V = arenaB[:, :].rearrange("p (t f) -> p t f", t=8)
    X1 = arenaB[:, :].bitcast(F32).rearrange("p (t f) -> p t f", t=4)
    QT = sb("QTz", [P, 8, 512], BF16)
    pooled = sb("pooled", [P, 4, 512], BF16)
    mT = sb("mT", [P, 8, 512], BF16)
    NPT = 6
    PT = [sb(f"PT{i}", [P, 512], BF16) for i in range(NPT)]
    tnh = [sb(f"tnh{i}", [P, 512], F32) for i in range(3)]
    t1 = sb("t1", [P, 512], F32)
    t1b = sb("t1b", [P, 512], F32)
    tmpS = [sb(f"tmpS{i}", [P, 512], F32) for i in range(2)]
    hb2 = [sb(f"hbx{i}", [P, D], BF16) for i in range(2)]
    t2 = sb("t2", [P, 512], F32)
    t3 = sb("t3", [P, 512], F32)
    rden = sb("rden", [P, 512], F32)
    pa = sb("pa", [P, 528], F32)
    pb = sb("pb", [P, 528], F32)
    stats = sb("stats", [P, 64], F32)
    negh = sb("negh", [P, 2], F32)
    ps = [nc.alloc_psum_tensor(f"ps{i}", [P, 512], F32) for i in range(8)]

    st_i = [0]

    def stat_col():
        st_i[0] = (st_i[0] + 1) % 64
        return st_i[0]

    mm_i = [0]

    def mm_bank():
        mm_i[0] = (mm_i[0] + 1) % 4
        return mm_i[0]

    acc_i = [0]

    def acc_pair():
        acc_i[0] = (acc_i[0] + 1) % 2
        return 4 + 2 * acc_i[0], 5 + 2 * acc_i[0]

    alt = [0]

    def evac_engine():
        alt[0] ^= 1
        return "act" if alt[0] else "dve"

    def copy_fn(engname, out, in_):
        if engname == "act":
            return lambda e: e.copy(out, in_)
        return lambda e: e.tensor_copy(out, in_)

    units = []
    for _b in range(nblk):
        for u in (0, 2, 3, 4, 1, 5, 7, 6, 8):
            units.append((s_win[u], ("scr", "win", u)))
        for u in range(2):
            units.append((s_wout[u], ("scr", "wout", u)))
        for u in range(11):
            units.append((s_wgu[u], ("scr", "wgu", u)))
        for _pss in range(2):
            for u in range(6):
                units.append((s_wd[u], ("scr", "wd", u)))
    wstate = {"issued": 0, "next": 0, "released": 0}

    def w_pump():
        while wstate["issued"] < len(units) and wstate["issued"] - NSLOT < wstate["released"]:
            j = wstate["issued"]
            slot = j % NSLOT
            src, key = units[j]
            ncol = 2048 if key == ("scr", "wd", 5) else 4096
            S.op("sp", (lambda e, slot=slot, src=src, ncol=ncol: e.dma_start(out=wring[slot][:, 0:ncol], in_=src[:, 0:ncol])),
                 reads=[key], writes=[("wr", slot)], dma_key=f"wr{slot}")
            wstate["issued"] += 1

    def w_acquire(hold_prev=0):
        j = wstate["next"]
        wstate["next"] += 1
        wstate["released"] = max(wstate["released"], j - hold_prev)
        w_pump()
        assert wstate["issued"] > j, (j, wstate)
        slot = j % NSLOT
        return slot, wring[slot]

    def pro_dma(out, in_, writes, grp="pro1"):
        S.op("pool", (lambda e: e.dma_start(out=out, in_=in_)), reads=[], writes=writes, dma_key=grp)

    pro_dma(wpool_sb[:, :, :], w_pool.rearrange("g c n -> c g n"), [("wpool",)], grp="pro0")
    pro_dma(ident_sb[:, :], ident, [("ident",)], grp="pro0")
    pro_dma(kone_sb[:, :], kone, [("kone",)], grp="pro0")
    w_in_v = w_in.rearrange("(k p) n -> p k n", p=P)
    for u in range(9):
        pro_dma(s_win[u].rearrange("p (k n) -> p k n", k=8), w_in_v[:, :, u * 512:(u + 1) * 512],
                [("scr", "win", u)], grp="pro0")
    w_out_v = w_out.rearrange("(k p) n -> p k n", p=P)
    for u in range(2):
        pro_dma(s_wout[u].rearrange("p (k n) -> p k n", k=8), w_out_v[:, :, u * 512:(u + 1) * 512],
                [("scr", "wout", u)])
    w_gu_v = w_gu.rearrange("(k p) n -> p k n", p=P)
    for u in range(11):
        dst = s_wgu[u].rearrange("p (a k n) -> p a k n", a=2, k=8)
        pro_dma(dst[:, 0], w_gu_v[:, :, u * 256:(u + 1) * 256], [("scr", "wgu", u, 0)])
        pro_dma(dst[:, 1], w_gu_v[:, :, DFF + u * 256:DFF + (u + 1) * 256], [("scr", "wgu", u, 1)])
    w_d_v = w_d.rearrange("(j p) n -> p j n", p=P)
    for u in range(6):
        nj = 4 if u < 5 else 2
        dst = s_wd[u].rearrange("p (j n) -> p j n", j=4)
        pro_dma(dst[:, 0:nj], w_d_v[:, 4 * u:4 * u + nj, :], [("scr", "wd", u)])
    for u in range(11):
        units_key = ("scr", "wgu", u)
        S.last_writer[units_key] = S.last_writer[("scr", "wgu", u, 1)]

    def sp_load(out, in_, writes, key):
        S.op("sp", (lambda e: e.dma_start(out=out, in_=in_)), reads=[], writes=writes, dma_key=key)

    sp_load(gtab[:, :, :], gains.rearrange("p (a n) -> p a n", a=4), [("gtab",)], "c0")
    sp_load(pscale_sb[:, :], pscale, [("pscale",)], "c1")
    sp_load(hmask_sb[:, :], hmask, [("hmask",)], "c2")
    sp_load(edge_sb[:, :], edge, [("edge",)], "c3")
    S.op("dve", (lambda e: e.memset(ones_sb[:, :], 1.0)), writes=[("ones",)])
    S.op("pool", (lambda e: e.memset(QT[:, :, :], 0.0)), writes=[("QT", h) for h in range(NH)])
    S.op("dve", (lambda e: e.memset(negh[:, :], -0.5)), writes=[("negh",)])
    S.op("dve", (lambda e: e.tensor_scalar(pscale_h[:, :], pscale_sb[:, :], 0.5, None, ALU.mult)),
         reads=[("pscale",)], writes=[("pscale_h",)])
    for h in range(NH):
        for half in range(2):
            i = (2 * h + half) % 2
            c0 = h * EBW + half * 704
            sp_load(xs[i][:, 0:704], btab[:, c0:c0 + 704], [("xs", i)], f"xs{i}")
            S.op("act", (lambda e, i=i, h=h, half=half: e.activation(
                out=ebt[:, h, half * 704:(half + 1) * 704], in_=xs[i][:, 0:704], func=AF.Copy, scale=8.0)),
                reads=[("xs", i)], writes=[("ebt", h, half)])
    EBT_R = [("ebt", h, half) for h in range(NH) for half in range(2)]

    def rstd_from(ss_ap, out_ap, reads, wkey):
        S.op("pool", (lambda e: e.tensor_scalar(out_ap, ss_ap, 1.0 / D, EPS, ALU.mult, ALU.add)),
             reads=reads, writes=[wkey])
        S.op("pool", (lambda e: e.tensor_tensor(out_ap, out_ap, negh[:, 0:1], ALU.pow)),
             reads=[wkey, ("negh",)], writes=[wkey])

    jk_i = [0]

    junkA = pa[:, :].bitcast(BF16)[:, 0:D]
    junkB = pb[:, :].bitcast(BF16)[:, 0:D]

    def junk_buf():
        jk_i[0] ^= 1
        return (junkA, ("pa",)) if jk_i[0] else (junkB, ("pb",))

    def norm_front(src_ap, src_reads, gidx, hbt, hbkey):
        c = stat_col()
        jb, jkey = junk_buf()
        S.op("act", (lambda e: e.activation(out=jb[:, :], in_=src_ap, func=AF.Square,
                                            accum_out=stats[:, c:c + 1])),
             reads=src_reads, writes=[("st", c), jkey])
        c2 = stat_col()
        rstd_from(stats[:, c:c + 1], stats[:, c2:c2 + 1], [("st", c)], ("st", c2))
        S.op("dve", (lambda e: e.scalar_tensor_tensor(hbt[:, :], src_ap, stats[:, c2:c2 + 1],
                                                       gtab[:, gidx, :], ALU.mult, ALU.mult)),
             reads=list(src_reads) + [("st", c2), ("gtab",)], writes=[hbkey])

    def norm_back(dstT, dst_keys, tcol0, hbt, hbkey):
        bank = mm_bank()
        psb = ps[bank][:, :].bitcast(BF16).rearrange("p (k t) -> p k t", k=8)
        S.op("pe", [(lambda e, k=k: e.transpose(psb[:, k, :], hbt[:, k * P:(k + 1) * P], ident_sb[:, :]))
                    for k in range(8)],
             reads=[hbkey, ("ident",)], writes=[("ps", bank)])
        en = evac_engine()
        S.op(en, copy_fn(en, dstT[:, :, tcol0:tcol0 + P], psb),
             reads=[("ps", bank)], writes=dst_keys)

    def norm_to_T(src_ap, src_reads, gidx, dstT, dst_key_fn, tcol0, hbt, hbkey):
        norm_front(src_ap, src_reads, gidx, hbt, hbkey)
        norm_back(dstT, dst_key_fn(), tcol0, hbt, hbkey)

    xs_i = [0]

    p1_slot = {}

    def phase1_front(bb, tt):
        rr = 512 * bb
        i = xs_i[0] = (xs_i[0] + 1) % 2
        p1_slot[(bb, tt)] = i
        sp_load(xs[i][:, :], xin[rr + tt * P:rr + (tt + 1) * P, :], [("xs", i)], f"xs{i}")
        norm_front(xs[i][:, :], [("xs", i)], 0, hb[i], ("hb", i))

    def phase1_back(bb, tt):
        i = p1_slot[(bb, tt)]
        norm_back(hT, [("hT", tt)], tt * P, hb[i], ("hb", i))

    def phase1_tile(bb, tt):
        phase1_front(bb, tt)
        phase1_back(bb, tt)

    for tt in range(8):
        phase1_tile(0, tt)

    for b in range(nblk):
        r0 = 512 * b
        hT_all = [("hT", t) for t in range(8)]
        hT_half = [[("hT", t) for t in range(4)], [("hT", t) for t in range(4, 8)]]
        hT_mid = [("hT", t) for t in range(2, 6)]

        first_A = [True]

        def tokA(write):
            return ([], [("tokA",)]) if write else ([("tokA",)], [])

        first_B = [True]

        def fm_proj(slot_t, col0, rhs_lo, nN, hreads):
            bank = mm_bank()
            wv = slot_t[:, :].rearrange("p (k n) -> p k n", k=8)
            return bank, [(lambda e, k=k: e.matmul(ps[bank][:, :], wv[:, k, col0:col0 + P],
                                                   hT[:, k, rhs_lo:rhs_lo + 512],
                                                   start=(k == 0), stop=(k == 7))) for k in range(8)]

        slot, wt = w_acquire()
        for c in range(4):
            for hh in range(2):
                bank, fns = fm_proj(wt, c * P, hh * 512, 512, None)
                S.op("pe", fns, reads=hT_half[hh] + [("wr", slot)], writes=[("ps", bank)])
                en = evac_engine()
                rd, wr = tokA(first_A[0])
                first_A[0] = False
                S.op(en, copy_fn(en, UT[:, c, hh * 512:(hh + 1) * 512], ps[bank][:, :]),
                     reads=[("ps", bank)] + rd, writes=[("UT", c, hh)] + wr)
        for g in range(4):
            w = (2, 4, 8, 16)[g]
            ukeys = [("UT", g, 0), ("UT", g, 1)]
            S.op("dve", (lambda e, g=g, b=b: e.tensor_scalar(UT[:, g, 248:256], UT[:, g, 248:256],
                                                            hmask_sb[:, 2 * b:2 * b + 1], None, ALU.mult)),
                 reads=ukeys + [("hmask",), ("tokA",)], writes=[("UT", g, 0)])
            S.op("dve", (lambda e, g=g, b=b: e.tensor_scalar(UT[:, g, 768:776], UT[:, g, 768:776],
                                                            hmask_sb[:, 2 * b + 1:2 * b + 2], None, ALU.mult)),
                 reads=ukeys + [("hmask",), ("tokA",)], writes=[("UT", g, 1)])
            S.op("dve", (lambda e, g=g: e.tensor_tensor(pa[:, 0:527], UT[:, g, 248:775], UT[:, g, 249:776], ALU.add)),
                 reads=ukeys + [("tokA",)], writes=[("pa",)])
            cur, curk, off = pa, ("pa",), 7
            if g >= 1:
                S.op("dve", (lambda e: e.tensor_tensor(pb[:, 0:525], pa[:, 0:525], pa[:, 2:527], ALU.add)),
                     reads=[("pa",)], writes=[("pb",)])
                cur, curk, off = pb, ("pb",), 6
            if g >= 2:
                S.op("dve", (lambda e: e.tensor_tensor(pa[:, 0:521], pb[:, 0:521], pb[:, 4:525], ALU.add)),
                     reads=[("pb",)], writes=[("pa",)])
                cur, curk, off = pa, ("pa",), 4
            if g >= 3:
                S.op("dve", (lambda e: e.tensor_tensor(pb[:, 0:513], pa[:, 0:513], pa[:, 8:521], ALU.add)),
                     reads=[("pa",)], writes=[("pb",)])
                cur, curk, off = pb, ("pb",), 0
            S.op("dve", (lambda e, cur=cur, off=off, w=w: e.tensor_scalar(
                t1[:, :], cur[:, off:off + 512], 1.0 / w, None, ALU.mult)),
                reads=[curk], writes=[("t1",)])
            e0 = b * 64 + g * 16
            S.op("dve", (lambda e, cur=cur, off=off, e0=e0: e.tensor_tensor(
                t1[:, 0:8], cur[:, off:off + 8], edge_sb[:, e0:e0 + 8], ALU.mult)),
                reads=[curk, ("edge",), ("t1",)], writes=[("t1",)])
            S.op("dve", (lambda e, cur=cur, off=off, e0=e0: e.tensor_tensor(
                t1[:, 504:512], cur[:, off + 504:off + 512], edge_sb[:, e0 + 8:e0 + 16], ALU.mult)),
                reads=[curk, ("edge",), ("t1",)], writes=[("t1",)])
            S.op("dve", (lambda e, g=g: e.tensor_tensor(pooled[:, g, :], t1[:, :], UT[:, g, 256:768], ALU.subtract)),
                 reads=[("t1",)] + ukeys + [("tokA",)], writes=[("pooled", g)])

        slot, wt = w_acquire()
        for c in range(4):
            for hh in range(2):
                bank, fns = fm_proj(wt, c * P, hh * 512, 512, None)
                S.op("pe", fns, reads=hT_half[hh] + [("wr", slot)], writes=[("ps", bank)])
                en = "act"
                S.op(en, copy_fn(en, KT[:, c, hh * 512:(hh + 1) * 512], ps[bank][:, :]),
                     reads=[("ps", bank), ("tokA",)], writes=[("KT", c, hh)])
        for cb in range(2):
            slot, wt = w_acquire()
            wv = wt[:, :].rearrange("p (k n) -> p k n", k=8)
            for tt in range(8):
                bank = mm_bank()
                S.op("pe", [(lambda e, k=k, tt=tt, bank=bank, wv=wv: e.matmul(
                    ps[bank][:, :], hT[:, k, tt * P:(tt + 1) * P], wv[:, k, :],
                    start=(k == 0), stop=(k == 7))) for k in range(8)],
                    reads=[("hT", tt), ("wr", slot)], writes=[("ps", bank)])
                en = evac_engine()
                if first_B[0]:
                    rd, wr = [], [("tokB",)]
                    first_B[0] = False
                else:
                    rd, wr = [("tokB",)], []
                S.op(en, copy_fn(en, V[:, tt, cb * 512:(cb + 1) * 512], ps[bank][:, :]),
                     reads=[("ps", bank)] + rd, writes=[("V", tt, cb)] + wr)
        slot, wt = w_acquire()
        for c in range(4):
            bank, fns = fm_proj(wt, c * P, 256, 512, None)
            S.op("pe", fns, reads=hT_mid + [("wr", slot)], writes=[("ps", bank)])
            for hp in range(2):
                en = evac_engine()
                S.op(en, copy_fn(en, QT[hp * 64:(hp + 1) * 64, 2 * c + hp, :], ps[bank][hp * 64:(hp + 1) * 64, :]),
                     reads=[("ps", bank)], writes=[("QT", 2 * c + hp)])
        qi = b % 2
        S.op("pool", (lambda e, qi=qi, b=b: e.dma_start(out=qmask_sb[qi][:, :], in_=qmask[b])),
             reads=[], writes=[("qmask", qi)], dma_key=f"qm{qi}")

        if debug and b == 0:
            def dump(name, src, reads):
                S.op("sp", (lambda e: e.dma_start(out=dbg[name], in_=src)), reads=reads, writes=[("dbg", name)],
                     dma_key="dbg")
            dump("d_hT", hT[:, :, :].rearrange("p k t -> p (k t)"), hT_all)
            dump("d_KT", arenaA[:, 8192:12288], [("KT", c, hh) for c in range(4) for hh in range(2)])
            dump("d_UT", arenaA[:, 0:8192].bitcast(F32), [("UT", c, hh) for c in range(4) for hh in range(2)] +
                 [("pooled", g) for g in range(4)])
            dump("d_V", arenaB[:, :], [("V", t, cb) for t in range(8) for cb in range(2)])
            pass
            dump("d_pooled", pooled[:, :, :].rearrange("p c t -> p (c t)"), [("pooled", g) for g in range(4)])

        e_i = [0]
        p_i = [0]
        gate_slots = {}
        head_acc = {}
        pt_of = {}
        LA = 4

        def head_pre(h):
            if h % 4 == 0:
                gate_slots["gp"] = w_acquire()
                gate_slots["ga"] = w_acquire(hold_prev=1)
            col = (h % 4) * P
            t1x = t1 if h % 2 == 0 else t1b
            t1k = ("t1", h % 2)
            gslot, gwt = gate_slots["gp"]
            bank, fns = fm_proj(gwt, col, 256, 512, None)
            S.op("pe", fns, reads=hT_mid + [("wr", gslot)], writes=[("ps", bank)])
            S.op("act", (lambda e, bank=bank: e.activation(out=tnh[2][:, :], in_=ps[bank][:, :], func=AF.Tanh, scale=0.5)),
                 reads=[("ps", bank)], writes=[("tnh", 2)])
            gslot, gwt = gate_slots["ga"]
            bank, fns = fm_proj(gwt, col, 256, 512, None)
            S.op("pe", fns, reads=hT_mid + [("wr", gslot)], writes=[("ps", bank)])
            ta = h % 2
            S.op("act", (lambda e, bank=bank, ta=ta: e.activation(out=tnh[ta][:, :], in_=ps[bank][:, :], func=AF.Tanh, scale=0.5)),
                 reads=[("ps", bank)], writes=[("tnh", ta)])
            g = h // 2
            abank = mm_bank()
            S.op("pe", (lambda e, abank=abank, g=g, h=h: e.matmul(
                ps[abank][:, :], wpool_sb[:, g, (h % 2) * P:(h % 2 + 1) * P], pooled[:, g, :], start=True, stop=True)),
                reads=[("wpool",), ("pooled", g)], writes=[("ps", abank)])
            S.op("act", (lambda e, abank=abank, h=h, t1x=t1x: e.activation(
                out=t1x[:, :], in_=ps[abank][:, :], func=AF.Copy, scale=pscale_h[:, h:h + 1])),
                reads=[("ps", abank), ("pscale_h",)], writes=[t1k])
            S.op("dve", (lambda e, t1x=t1x: e.scalar_tensor_tensor(t1x[:, :], tnh[2][:, :], 1.0, t1x[:, :], ALU.add, ALU.mult)),
                 reads=[t1k, ("tnh", 2)], writes=[t1k])

        def rec_S(h, p):
            ch = h // 2
            pl = (h % 2) * 64
            bank = mm_bank()
            c0 = (14 - 2 * p) * 64
            S.op("pe", [
                (lambda e, bank=bank, p=p, h=h, ch=ch: e.matmul(
                    ps[bank][:, :], KT[:, ch, p * P:(p + 1) * P],
                    QT[:, h, :], start=True, stop=False)),
                (lambda e, bank=bank, p=p, qi=qi: e.matmul(
                    ps[bank][:, :], kone_sb[:, p * P:(p + 1) * P],
                    qmask_sb[qi][:, :], start=False, stop=False)),
                (lambda e, bank=bank, c0=c0, h=h: e.matmul(
                    ps[bank][:, :], ident_sb[:, :], ebt[:, h, c0:c0 + 512], start=False, stop=True)),
            ], reads=[("KT", ch, p // 4), ("QT", h), ("kone",), ("qmask", qi), ("tokA",), ("ident",)] + EBT_R[2 * h:2 * h + 2],
                writes=[("ps", bank)])
            pi = p_i[0] = (p_i[0] + 1) % NPT
            S.op("act", (lambda e, bank=bank, pi=pi: e.activation(out=PT[pi][:, :], in_=ps[bank][:, :],
                                                                   func=AF.Exp, scale=0.125)),
                 reads=[("ps", bank)], writes=[("PT", pi)])
            pt_of[(h, p)] = pi

        def rec_PV(h, p):
            if p == 0:
                head_acc[h] = acc_pair()
            pvb, dnb = head_acc[h]
            pi = pt_of[(h, p)]
            S.op("pe", [
                (lambda e, pi=pi, p=p, h=h, pvb=pvb: e.matmul(ps[pvb][:, :], V[:, p, h * P:(h + 1) * P],
                                                              PT[pi][:, :], start=(p == 0), stop=(p == 7))),
                (lambda e, pi=pi, p=p, dnb=dnb: e.matmul(ps[dnb][:, :], ones_sb[:, :], PT[pi][:, :],
                                                         start=(p == 0), stop=(p == 7))),
            ], reads=[("PT", pi), ("V", p, h // 4), ("ones",), ("tokB",)],
                writes=[("ps", pvb), ("ps", dnb)])

        def head_post(h):
            pvb, dnb = head_acc[h]
            t1x = t1 if h % 2 == 0 else t1b
            t1k = ("t1", h % 2)
            ta = h % 2
            S.op("dve", (lambda e, dnb=dnb: e.reciprocal(rden[:, :], ps[dnb][:, :])),
                 reads=[("ps", dnb)], writes=[("rden",)])
            S.op("dve", (lambda e, pvb=pvb: e.scalar_tensor_tensor(t2[:, :], ps[pvb][:, :], 0.5, rden[:, :], ALU.mult, ALU.mult)),
                 reads=[("ps", pvb), ("rden",)], writes=[("t2",)])
            S.op("dve", (lambda e, ta=ta: e.scalar_tensor_tensor(t3[:, :], tnh[ta][:, :], 1.0, t2[:, :], ALU.add, ALU.mult)),
                 reads=[("t2",), ("tnh", ta)], writes=[("t3",)])
            S.op("pool", (lambda e, h=h, t1x=t1x: e.tensor_tensor(mT[:, h, :], t1x[:, :], t3[:, :], ALU.add)),
                 reads=[t1k, ("t3",)], writes=[("mT", h, t) for t in range(4)])

        tiles = [(h, p) for h in range(NH) for p in range(8)]
        for idx in range(len(tiles) + LA):
            if idx < len(tiles):
                h, p = tiles[idx]
                if p == 0:
                    head_pre(h)
                rec_S(h, p)
            j = idx - LA
            if j >= 0:
                h, p = tiles[j]
                rec_PV(h, p)
                if p == 7:
                    head_post(h)
        mT_all = [("mT", f, t) for f in range(8) for t in range(4)]
        if debug and b == 0:
            dump("d_mT", mT[:, :, :].rearrange("p c t -> p (c t)"), mT_all)

        S.op("sp", (lambda e, r0=r0: e.dma_start(
            out=X1[:, :, :], in_=xin[r0 + 256:r0 + 768, :].rearrange("(t p) f -> p t f", p=P))),
            reads=[], writes=[("tokB",)] + [("x1", t) for t in range(4)], dma_key="x1")
        wo = [w_acquire(), w_acquire(hold_prev=1)]
        p5 = {}

        def p5_A1(tb):
            banks = acc_pair()
            for cb in range(2):
                oslot, owt = wo[cb]
                wv = owt[:, :].rearrange("p (k n) -> p k n", k=8)
                S.op("pe", [(lambda e, f=f, cb=cb, tb=tb, wv=wv, bk=banks[cb]: e.matmul(
                    ps[bk][:, :], mT[:, f, tb * P:(tb + 1) * P], wv[:, f, :], start=(f == 0), stop=(f == 7)))
                    for f in range(8)],
                    reads=[("mT", f, tb) for f in range(8)] + [("wr", oslot)], writes=[("ps", banks[cb])])
            ca, cbb, cs = stat_col(), stat_col(), stat_col()
            for cb, cc in ((0, ca), (1, cbb)):
                jb, jkey = junk_buf()
                S.op("act", (lambda e, cb=cb, cc=cc, banks=banks, jb=jb: e.activation(
                    out=jb[:, 0:512], in_=ps[banks[cb]][:, :], func=AF.Square, accum_out=stats[:, cc:cc + 1])),
                    reads=[("ps", banks[cb])], writes=[("st", cc), jkey])
            S.op("dve", (lambda e, ca=ca, cbb=cbb, cs=cs: e.tensor_tensor(
                stats[:, cs:cs + 1], stats[:, ca:ca + 1], stats[:, cbb:cbb + 1], ALU.add)),
                reads=[("st", ca), ("st", cbb)], writes=[("st", cs)])
            p5[tb] = (banks, cs)

        def p5_A2(tb):
            banks, cs = p5[tb]
            cr = stat_col()
            rstd_from(stats[:, cs:cs + 1], stats[:, cr:cr + 1], [("st", cs)], ("st", cr))
            for cb in range(2):
                tmp = tmpS[cb]
                S.op("dve", (lambda e, cb=cb, cr=cr, banks=banks, tmp=tmp: e.scalar_tensor_tensor(
                    tmp[:, 0:512], ps[banks[cb]][:, :], stats[:, cr:cr + 1], gtab[:, 1, cb * 512:(cb + 1) * 512],
                    ALU.mult, ALU.mult)),
                    reads=[("ps", banks[cb]), ("st", cr), ("gtab",)], writes=[("tmpS", cb)])
                S.op("dve", (lambda e, cb=cb, tb=tb, tmp=tmp: e.tensor_tensor(
                    X1[:, tb, cb * 512:(cb + 1) * 512], X1[:, tb, cb * 512:(cb + 1) * 512], tmp[:, 0:512], ALU.add)),
                    reads=[("tmpS", cb), ("x1", tb), ("tokB",)], writes=[("x1", tb)])

        def p5_F(tb):
            norm_front(X1[:, tb, :], [("x1", tb), ("tokB",)], 2, hb2[tb % 2], ("hbx", tb % 2))

        def p5_Bk(tb):
            norm_back(mT, [("mT", f, tb) for f in range(8)], tb * P, hb2[tb % 2], ("hbx", tb % 2))

        p5_A1(0)
        p5_A1(1)
        p5_A2(0)
        p5_F(0)
        p5_A1(2)
        p5_A2(1)
        p5_Bk(0)
        p5_F(1)
        p5_A1(3)
        p5_A2(2)
        p5_Bk(1)
        p5_F(2)
        p5_A2(3)
        p5_Bk(2)
        p5_F(3)
        p5_Bk(3)
        h2_all = mT_all
        if debug and b == 0:
            dump("d_h2T", mT[:, :, :].rearrange("p c t -> p (c t)"), h2_all)
            dump("d_x1", arenaB[:, :].bitcast(F32), [("x1", t) for t in range(4)] + h2_all)

        first_act = [True]
        for u in range(11):
            slot, wt = w_acquire()
            wv = wt[:, :].rearrange("p (a k n) -> p a k n", a=2, k=8)
            for jj in range(2):
                j = 2 * u + jj
                gb = mm_bank()
                S.op("pe", [(lambda e, k=k, gb=gb, jj=jj, wv=wv: e.matmul(
                    ps[gb][:, :], wv[:, 0, k, jj * P:(jj + 1) * P], mT[:, k, :], start=(k == 0), stop=(k == 7)))
                    for k in range(8)],
                    reads=h2_all + [("wr", slot)], writes=[("ps", gb)])
                ub = mm_bank()
                S.op("pe", [(lambda e, k=k, ub=ub, jj=jj, wv=wv: e.matmul(
                    ps[ub][:, :], wv[:, 1, k, jj * P:(jj + 1) * P], mT[:, k, :], start=(k == 0), stop=(k == 7)))
                    for k in range(8)],
                    reads=h2_all + [("wr", slot)], writes=[("ps", ub)])
                ti = j % 2
                S.op("act", (lambda e, gb=gb, ti=ti: e.activation(out=tnh[ti][:, :], in_=ps[gb][:, :], func=AF.Tanh, scale=0.5)),
                     reads=[("ps", gb)], writes=[("tnh", ti)])
                tw = t1 if ti == 0 else t1b
                twk = ("t1", ti)
                S.op("dve", (lambda e, gb=gb, ti=ti, tw=tw: e.scalar_tensor_tensor(
                    tw[:, :], tnh[ti][:, :], 1.0, ps[gb][:, :], ALU.add, ALU.mult)),
                    reads=[("tnh", ti), ("ps", gb)], writes=[twk])
                if first_act[0]:
                    rd, wr = [], [("tokA",)]
                    first_act[0] = False
                else:
                    rd, wr = [("tokA",)], []
                S.op("dve", (lambda e, ub=ub, tw=tw, j=j: e.scalar_tensor_tensor(
                    ACTT[:, j, :], tw[:, :], 0.5, ps[ub][:, :], ALU.mult, ALU.mult)),
                    reads=[twk, ("ps", ub)] + rd, writes=[("actT", j)] + wr)
            if b + 1 < nblk:
                if 2 <= u <= 9:
                    phase1_back(b + 1, u - 2)
                if 1 <= u <= 8:
                    phase1_front(b + 1, u - 1)
        act_all = [("actT", j) for j in range(22)]
        for pss in range(2):
            tbs = (2 * pss, 2 * pss + 1)
            bankmap = {tbs[0]: (4, 5), tbs[1]: (6, 7)}
            for u in range(6):
                slot, wt = w_acquire()
                nj = 4 if u < 5 else 2
                wv = wt[:, :].rearrange("p (j n) -> p j n", j=4)
                for tb in tbs:
                    for cb in range(2):
                        bk = bankmap[tb][cb]
                        S.op("pe", [(lambda e, jl=jl, u=u, tb=tb, cb=cb, bk=bk, wv=wv, nj=nj: e.matmul(
                            ps[bk][:, :], ACTT[:, 4 * u + jl, tb * P:(tb + 1) * P], wv[:, jl, cb * 512:(cb + 1) * 512],
                            start=(u == 0 and jl == 0), stop=(u == 5 and jl == nj - 1))) for jl in range(nj)],
                            reads=act_all + [("wr", slot), ("tokA",)], writes=[("ps", bk)])
            for tb in tbs:
                banks = bankmap[tb]
                ca, cbb, cs = stat_col(), stat_col(), stat_col()
                for cb, cc in ((0, ca), (1, cbb)):
                    jb, jkey = junk_buf()
                    S.op("act", (lambda e, cb=cb, cc=cc, banks=banks, jb=jb: e.activation(
                        out=jb[:, 0:512], in_=ps[banks[cb]][:, :], func=AF.Square, accum_out=stats[:, cc:cc + 1])),
                        reads=[("ps", banks[cb])], writes=[("st", cc), jkey])
                S.op("dve", (lambda e, ca=ca, cbb=cbb, cs=cs: e.tensor_tensor(
                    stats[:, cs:cs + 1], stats[:, ca:ca + 1], stats[:, cbb:cbb + 1], ALU.add)),
                    reads=[("st", ca), ("st", cbb)], writes=[("st", cs)])
                cr = stat_col()
                rstd_from(stats[:, cs:cs + 1], stats[:, cr:cr + 1], [("st", cs)], ("st", cr))
                for cb in range(2):
                    tmp = tmpS[cb]
                    S.op("dve", (lambda e, cb=cb, cr=cr, banks=banks, tmp=tmp: e.scalar_tensor_tensor(
                        tmp[:, 0:512], ps[banks[cb]][:, :], stats[:, cr:cr + 1], gtab[:, 3, cb * 512:(cb + 1) * 512],
                        ALU.mult, ALU.mult)),
                        reads=[("ps", banks[cb]), ("st", cr), ("gtab",)], writes=[("tmpS", cb)])
                    S.op("dve", (lambda e, cb=cb, tb=tb, tmp=tmp: e.tensor_tensor(
                        X1[:, tb, cb * 512:(cb + 1) * 512], X1[:, tb, cb * 512:(cb + 1) * 512], tmp[:, 0:512], ALU.add)),
                        reads=[("tmpS", cb), ("x1", tb), ("tokB",)], writes=[("x1", tb)])
                S.op("sp", (lambda e, tb=tb, b=b: e.dma_start(
                    out=yout[512 * b + tb * P:512 * b + (tb + 1) * P, :], in_=X1[:, tb, :])),
                    reads=[("x1", tb), ("tokB",)], writes=[("yout", b, tb)], dma_key="y")

    S.op("sp", [], reads=[], writes=[("yout", b, tb) for b in range(nblk) for tb in range(4)] +
         ([("dbg", k) for k in dbg] if debug else []))

    keys = set()
    for o in S.ops:
        if o.dma_key is not None:
            keys.add("dma:" + o.dma_key)
    for e in Sched.ENGS:
        keys.add("eng:" + e)
    sems = {k: nc.alloc_semaphore(k.replace(":", "_")) for k in sorted(keys)}
    S.finalize(nc, sems)
    with nc.Block() as block:
        @block.sync
        def _(eng):
            S.emit("sp", eng)

        @block.scalar
        def _(eng):
            S.emit("act", eng)

        @block.vector
        def _(eng):
            S.emit("dve", eng)

        @block.gpsimd
        def _(eng):
            S.emit("pool", eng)

        @block.tensor
        def _(eng):
            S.emit("pe", eng)
    return nc, S


def host_tables(attn_rpb, nblk, core, blk0=0):
    qmask = np.zeros((nblk, P, 512), np.float32)
    qmask[:, 0:16, :] = NEG
    edge = np.zeros((nblk, 4, 16), np.float32)
    hm = np.zeros((nblk, 2), np.float32)
    for b in range(nblk):
        G0 = core * TOK_CORE + 512 * (b + blk0)
        s0 = G0 % SEQ
        r0 = s0 // 64
        for i in range(8):
            r = r0 + i
            rs = min(max(r - 4, 0), 120)
            for u in range(16):
                kr = r0 - 4 + u
                if rs <= kr < rs + 8:
                    qmask[b, u, i * 64:(i + 1) * 64] = 0.0
        for g, w in enumerate((2, 4, 8, 16)):
            for j in range(16):
                t = j if j < 8 else 504 + (j - 8)
                s = s0 + t
                lo = max(s - w // 2, 0)
                hi = min(s + w // 2, SEQ)
                edge[b, g, j] = 1.0 / float(hi - lo)
        hm[b, 0] = 0.0 if s0 == 0 else 1.0
        hm[b, 1] = 0.0 if s0 + 512 == SEQ else 1.0
    edge_r = np.ascontiguousarray(np.broadcast_to(edge.reshape(1, -1), (P, nblk * 64)))
    hm_r = np.ascontiguousarray(np.broadcast_to(hm.reshape(1, -1), (P, nblk * 2)))
    return qmask, edge_r, hm_r


def bias_table(attn_rpb):
    rpb = np.asarray(attn_rpb, np.float32).reshape(NH, 15, 31)
    e = np.arange(2)[:, None, None, None]
    j = np.arange(64)[None, :, None, None]
    mm = np.arange(MMW)[None, None, :, None]
    c = np.arange(64)[None, None, None, :]
    dr = e + 10 - mm
    cs = np.clip(c - 8, 0, 48)
    ok = (np.abs(dr) <= 7) & (j >= cs) & (j < cs + 16)
    dri = np.clip(dr + 7, 0, 14)
    dci = np.clip(j - c + 15, 0, 30)
    dri_b = np.broadcast_to(dri, (2, 64, MMW, 64))
    dci_b = np.broadcast_to(dci, (2, 64, MMW, 64))
    ok_b = np.broadcast_to(ok, (2, 64, MMW, 64))
    out = np.empty((P, NH, EBW), np.float32)
    for h in range(NH):
        vals = rpb[h][dri_b, dci_b]
        tab = np.where(ok_b, vals, np.float32(NEG)).astype(np.float32)
        out[:, h, :] = tab.reshape(P, EBW)
    return out.reshape(P, NH * EBW)


def make_in_maps(inputs, nblk, blk0=0):
    xp = np.asarray(inputs["x_prompt"], np.float32).reshape(-1, D)
    xsm = np.asarray(inputs["x_sample"], np.float32).reshape(-1, D)
    xall = np.concatenate([xp, xsm], axis=0)
    ntot = xall.shape[0]
    gains = np.concatenate([np.asarray(inputs[k], np.float32).reshape(1, D) for k in
                            ("norm_mix_pre", "norm_mix_post", "norm_ffn_pre", "norm_ffn_post")], axis=1)
    gains_r = np.ascontiguousarray(np.broadcast_to(gains, (P, 4 * D)))
    pscale = np.ascontiguousarray(np.asarray(inputs["pool_scale"], np.float32).reshape(8, P).T)
    btab = bias_table(inputs["attn_rpb"])
    kone = np.zeros((P, 1024), np.float32)
    for u in range(16):
        kone[u, u * 64:(u + 1) * 64] = 1.0
    ident = np.eye(P, dtype=np.float32)
    shared = {
        "w_in": np.ascontiguousarray(np.asarray(inputs["w_in"], np.float32).reshape(D, 4608)),
        "w_gu": np.ascontiguousarray(np.asarray(inputs["w_gate_up"], np.float32).reshape(D, 2 * DFF)),
        "w_d": np.ascontiguousarray(np.asarray(inputs["w_down"], np.float32).reshape(DFF, D)),
        "w_out": np.ascontiguousarray(np.asarray(inputs["w_out"], np.float32).reshape(D, D)),
        "w_pool": np.ascontiguousarray(np.asarray(inputs["w_pool_grp"], np.float32).reshape(4, 128, 256)),
        "gains": gains_r, "pscale": pscale, "btab": btab, "kone": kone, "ident": ident,
    }
    in_maps = []
    ntok_in = 512 * nblk + 512
    for c in range(NCORE):
        lo = c * TOK_CORE + 512 * blk0 - 256
        hi = lo + ntok_in
        xin = np.zeros((ntok_in, D), np.float32)
        a, bnd = max(lo, 0), min(hi, ntot)
        xin[a - lo:bnd - lo] = xall[a:bnd]
        qm, edge_r, hm_r = host_tables(inputs["attn_rpb"], nblk, c, blk0)
        m = dict(shared)
        m.update({"xin": xin, "qmask": qm, "edge": edge_r, "hmask": hm_r})
        in_maps.append(m)
    return in_maps


_CACHE = {}


def kernel(x_prompt, x_sample, norm_mix_pre, w_in, w_pool_grp, pool_scale, attn_rpb, w_out,
           norm_mix_post, norm_ffn_pre, w_gate_up, w_down, norm_ffn_post):
    inputs = dict(x_prompt=x_prompt, x_sample=x_sample, norm_mix_pre=norm_mix_pre, w_in=w_in,
                  w_pool_grp=w_pool_grp, pool_scale=pool_scale, attn_rpb=attn_rpb, w_out=w_out,
                  norm_mix_post=norm_mix_post, norm_ffn_pre=norm_ffn_pre, w_gate_up=w_gate_up,
                  w_down=w_down, norm_ffn_post=norm_ffn_post)
    nblk = NBLK_FULL
    in_maps = make_in_maps(inputs, nblk)
    nc, _ = build(nblk)
    res = run_bass_kernel_spmd(nc, in_maps, core_ids=list(range(NCORE)))
    y = np.concatenate([np.asarray(r["yout"], np.float32) for r in res.results], axis=0)
    nprompt = np.asarray(x_prompt).shape[0] * np.asarray(x_prompt).shape[1]
    y_prompt = y[:nprompt].reshape(np.asarray(x_prompt).shape).astype(np.float32)
    y_sample = y[nprompt:].reshape(np.asarray(x_sample).shape).astype(np.float32)
    return (y_prompt, y_sample)
```

```python
import numpy as np
import concourse.bass as bass
import concourse.mybir as mybir
from concourse.bass_utils import run_bass_kernel_spmd

F32 = mybir.dt.float32
BF16 = mybir.dt.bfloat16
AF = mybir.ActivationFunctionType
ALU = mybir.AluOpType

P = 128
D = 1024
DFF = 2816
NH = 8
NCORE = 8
TOK_CORE = 10240
NBLK_FULL = 20
SEQ = 8192
MMW = 22
EBW = MMW * 64
EPS = 1e-6
NEG = -30000.0
NSLOT = 4


class Op:
    __slots__ = ("eng", "fns", "deps", "dma_key", "sig", "sem", "val", "idx")


class Sched:
    ENGS = ("pe", "act", "dve", "pool", "sp")

    def __init__(self):
        self.ops = []
        self.last_writer = {}
        self.readers = {}
        self.dma_count = {}

    def op(self, eng, fns, reads=(), writes=(), dma_key=None):
        o = Op()
        o.idx = len(self.ops)
        o.eng = eng
        o.fns = fns if isinstance(fns, (list, tuple)) else [fns]
        deps = set()
        for r in reads:
            w = self.last_writer.get(r)
            if w is not None:
                deps.add(w)
        for r in writes:
            w = self.last_writer.get(r)
            if w is not None:
                deps.add(w)
            deps.update(self.readers.get(r, ()))
        for r in reads:
            self.readers.setdefault(r, []).append(o.idx)
        for r in writes:
            self.last_writer[r] = o.idx
            self.readers[r] = []
        o.deps = deps
        o.dma_key = dma_key
        o.sig = False
        o.sem = None
        o.val = 0
        self.ops.append(o)
        return o

    def finalize(self, nc, sems):
        ops = self.ops
        for o in ops:
            for d in o.deps:
                od = ops[d]
                if od.eng == "pe" and o.eng == "pe" and od.dma_key is None:
                    continue
                od.sig = True
        cnt = {e: 0 for e in self.ENGS}
        dcnt = {}
        for o in ops:
            if o.dma_key is not None:
                dcnt[o.dma_key] = dcnt.get(o.dma_key, 0) + 1
                o.sem = sems["dma:" + o.dma_key]
                o.val = 16 * dcnt[o.dma_key]
                o.sig = True
            elif o.sig:
                cnt[o.eng] += 1
                o.sem = sems["eng:" + o.eng]
                o.val = cnt[o.eng]
        for o in ops:
            if o.dma_key in ("pro0", "pro1", "pro2"):
                o.val = 16 * dcnt[o.dma_key]
        self.max_counts = dict(cnt)
        self.dma_counts = dcnt

    def emit(self, eng_name, eng):
        ops = self.ops
        waited = {}
        for o in ops:
            if o.eng != eng_name:
                continue
            need = {}
            for d in o.deps:
                od = ops[d]
                if od.eng == "pe" and o.eng == "pe" and od.dma_key is None:
                    continue
                k = id(od.sem)
                if k not in need or need[k][1] < od.val:
                    need[k] = (od.sem, od.val)
            for k, (sem, val) in need.items():
                if waited.get(k, 0) >= val:
                    continue
                eng.wait_ge(sem, val)
                waited[k] = val
            last = None
            for fn in o.fns:
                last = fn(eng)
            if o.sig and last is not None:
                last.then_inc(o.sem, 16 if o.dma_key is not None else 1)


def build(nblk, debug=False):
    nc = bass.Bass("TRN2", target_bir_lowering=False)
    S = Sched()
    ntok_in = 512 * nblk + 512

    def din(name, shape, dt=F32):
        return nc.dram_tensor(name, list(shape), dt, kind="ExternalInput").ap()

    xin = din("xin", [ntok_in, D])
    w_in = din("w_in", [D, 4608])
    w_gu = din("w_gu", [D, 2 * DFF])
    w_d = din("w_d", [DFF, D])
    w_out = din("w_out", [D, D])
    w_pool = din("w_pool", [4, 128, 256])
    gains = din("gains", [P, 4 * D])
    pscale = din("pscale", [P, 8])
    btab = din("btab", [P, NH * EBW])
    qmask = din("qmask", [nblk, P, 512])
    kone = din("kone", [P, 1024])
    edge = din("edge", [P, nblk * 64])
    hmask = din("hmask", [P, nblk * 2])
    ident = din("ident", [P, P])
    yout = nc.dram_tensor("yout", [512 * nblk, D], F32, kind="ExternalOutput").ap()
    dbg = {}
    if debug:
        for nm, shp in (("d_hT", [P, 8 * 1024]), ("d_KT", [P, 4 * 1024]), ("d_V", [P, 8 * 1024]),
                        ("d_QT", [P, 4 * 512]), ("d_pooled", [P, 4 * 512]), ("d_mT", [P, 8 * 512]),
                        ("d_h2T", [P, 8 * 512])):
            dbg[nm] = nc.dram_tensor(nm, shp, BF16, kind="ExternalOutput").ap()
        dbg["d_x1"] = nc.dram_tensor("d_x1", [P, 4 * 1024], F32, kind="ExternalOutput").ap()
        dbg["d_UT"] = nc.dram_tensor("d_UT", [P, 4 * 1024], F32, kind="ExternalOutput").ap()

    s_win = nc.dram_tensor("s_win", [9, P, 4096], BF16).ap()
    s_wout = nc.dram_tensor("s_wout", [2, P, 4096], BF16).ap()
    s_wgu = nc.dram_tensor("s_wgu", [11, P, 4096], BF16).ap()
    s_wd = nc.dram_tensor("s_wd", [6, P, 4096], BF16).ap()

    def sb(name, shape, dt):
        return nc.alloc_sbuf_tensor(name, list(shape), dt)

    wring = [sb(f"wr{i}", [P, 4096], BF16) for i in range(NSLOT)]
    wpool_sb = sb("wpool", [P, 4, 256], BF16)
    ebt = sb("ebt", [P, NH, EBW], BF16)
    gtab = sb("gtab", [P, 4, D], F32)
    pscale_sb = sb("pscale_sb", [P, 8], F32)
    pscale_h = sb("pscale_h", [P, 8], F32)
    hmask_sb = sb("hmask_sb", [P, nblk * 2], F32)
    edge_sb = sb("edge_sb", [P, nblk * 64], F32)
    ident_sb = sb("ident_sb", [P, P], BF16)
    ones_sb = sb("ones_sb", [P, P], BF16)
    kone_sb = sb("kone_sb", [P, 1024], BF16)
    qmask_sb = [sb(f"qmask{i}", [P, 512], BF16) for i in range(2)]
    xs = [sb(f"xs{i}", [P, D], F32) for i in range(2)]
    hb = [sb(f"hb{i}", [P, D], BF16) for i in range(2)]
    hT = sb("hT", [P, 8, 1024], BF16)
    arenaA = sb("arenaA", [P, 12288], BF16)
    UT = arenaA[:, 0:8192].bitcast(F32).rearrange("p (c t) -> p c t", c=4)
    KT = arenaA[:, 8192:12288].rearrange("p (c t) -> p c t", c=4)
    ACTT = arenaA[:, 0:22 * 512].rearrange("p (j t) -> p j t", j=22)
    arenaB = sb("arenaB", [P, 8192], BF16)
    V = arenaB[:, :].rearrange("p (t f) -> p t f", t=8)
    X1 = arenaB[:, :].bitcast(F32).rearrange("p (t f) -> p t f", t=4)
    QT = sb("QTz", [P, 8, 512], BF16)
    pooled = sb("pooled", [P, 4, 512], BF16)
    mT = sb("mT", [P, 8, 512], BF16)
    NPT = 6
    PT = [sb(f"PT{i}", [P, 512], BF16) for i in range(NPT)]
    tnh = [sb(f"tnh{i}", [P, 512], F32) for i in range(3)]
    t1 = sb("t1", [P, 512], F32)
    t1b = sb("t1b", [P, 512], F32)
    tmpS = [sb(f"tmpS{i}", [P, 512], F32) for i in range(2)]
    hb2 = [sb(f"hbx{i}", [P, D], BF16) for i in range(2)]
    t2 = sb("t2", [P, 512], F32)
    t3 = sb("t3", [P, 512], F32)
    rden = sb("rden", [P, 512], F32)
    pa = sb("pa", [P, 528], F32)
    pb = sb("pb", [P, 528], F32)
    stats = sb("stats", [P, 64], F32)
    negh = sb("negh", [P, 2], F32)
    ps = [nc.alloc_psum_tensor(f"ps{i}", [P, 512], F32) for i in range(8)]

    st_i = [0]

    def stat_col():
        st_i[0] = (st_i[0] + 1) % 64
        return st_i[0]

    mm_i = [0]

    def mm_bank():
        mm_i[0] = (mm_i[0] + 1) % 4
        return mm_i[0]

    acc_i = [0]

    def acc_pair():
        acc_i[0] = (acc_i[0] + 1) % 2
        return 4 + 2 * acc_i[0], 5 + 2 * acc_i[0]

    alt = [0]

    def evac_engine():
        alt[0] ^= 1
        return "act" if alt[0] else "dve"

    def copy_fn(engname, out, in_):
        if engname == "act":
            return lambda e: e.copy(out, in_)
        return lambda e: e.tensor_copy(out, in_)

    units = []
    for _b in range(nblk):
        for u in (0, 2, 3, 4, 1, 5, 7, 6, 8):
            units.append((s_win[u], ("scr", "win", u)))
        for u in range(2):
            units.append((s_wout[u], ("scr", "wout", u)))
        for u in range(11):
            units.append((s_wgu[u], ("scr", "wgu", u)))
        for _pss in range(2):
            for u in range(6):
                units.append((s_wd[u], ("scr", "wd", u)))
    wstate = {"issued": 0, "next": 0, "released": 0}

    def w_pump():
        while wstate["issued"] < len(units) and wstate["issued"] - NSLOT < wstate["released"]:
            j = wstate["issued"]
            slot = j % NSLOT
            src, key = units[j]
            ncol = 2048 if key == ("scr", "wd", 5) else 4096
            S.op("sp", (lambda e, slot=slot, src=src, ncol=ncol: e.dma_start(out=wring[slot][:, 0:ncol], in_=src[:, 0:ncol])),
                 reads=[key], writes=[("wr", slot)], dma_key=f"wr{slot}")
            wstate["issued"] += 1

    def w_acquire(hold_prev=0):
        j = wstate["next"]
        wstate["next"] += 1
        wstate["released"] = max(wstate["released"], j - hold_prev)
        w_pump()
        assert wstate["issued"] > j, (j, wstate)
        slot = j % NSLOT
        return slot, wring[slot]

    def pro_dma(out, in_, writes, grp="pro1"):
        S.op("pool", (lambda e: e.dma_start(out=out, in_=in_)), reads=[], writes=writes, dma_key=grp)

    pro_dma(wpool_sb[:, :, :], w_pool.rearrange("g c n -> c g n"), [("wpool",)], grp="pro2")
    pro_dma(ident_sb[:, :], ident, [("ident",)], grp="pro2")
    pro_dma(kone_sb[:, :], kone, [("kone",)], grp="pro2")
    w_in_v = w_in.rearrange("(k p) n -> p k n", p=P)
    for u in range(9):
        pro_dma(s_win[u].rearrange("p (k n) -> p k n", k=8), w_in_v[:, :, u * 512:(u + 1) * 512],
                [("scr", "win", u)], grp="pro0")
    w_out_v = w_out.rearrange("(k p) n -> p k n", p=P)
    for u in range(2):
        pro_dma(s_wout[u].rearrange("p (k n) -> p k n", k=8), w_out_v[:, :, u * 512:(u + 1) * 512],
                [("scr", "wout", u)])
    w_gu_v = w_gu.rearrange("(k p) n -> p k n", p=P)
    for u in range(11):
        dst = s_wgu[u].rearrange("p (a k n) -> p a k n", a=2, k=8)
        pro_dma(dst[:, 0], w_gu_v[:, :, u * 256:(u + 1) * 256], [("scr", "wgu", u, 0)])
        pro_dma(dst[:, 1], w_gu_v[:, :, DFF + u * 256:DFF + (u + 1) * 256], [("scr", "wgu", u, 1)])
    w_d_v = w_d.rearrange("(j p) n -> p j n", p=P)
    for u in range(6):
        nj = 4 if u < 5 else 2
        dst = s_wd[u].rearrange("p (j n) -> p j n", j=4)
        pro_dma(dst[:, 0:nj], w_d_v[:, 4 * u:4 * u + nj, :], [("scr", "wd", u)])
    for u in range(11):
        units_key = ("scr", "wgu", u)
        S.last_writer[units_key] = S.last_writer[("scr", "wgu", u, 1)]

    def sp_load(out, in_, writes, key):
        S.op("sp", (lambda e: e.dma_start(out=out, in_=in_)), reads=[], writes=writes, dma_key=key)

    sp_load(gtab[:, :, :], gains.rearrange("p (a n) -> p a n", a=4), [("gtab",)], "c0")
    sp_load(pscale_sb[:, :], pscale, [("pscale",)], "c1")
    sp_load(hmask_sb[:, :], hmask, [("hmask",)], "c2")
    sp_load(edge_sb[:, :], edge, [("edge",)], "c3")
    S.op("dve", (lambda e: e.memset(ones_sb[:, :], 1.0)), writes=[("ones",)])
    S.op("pool", (lambda e: e.memset(QT[:, :, :], 0.0)), writes=[("QT", h) for h in range(NH)])
    S.op("dve", (lambda e: e.memset(negh[:, :], -0.5)), writes=[("negh",)])
    S.op("dve", (lambda e: e.tensor_scalar(pscale_h[:, :], pscale_sb[:, :], 0.5, None, ALU.mult)),
         reads=[("pscale",)], writes=[("pscale_h",)])
    for h in range(NH):
        for half in range(2):
            i = (2 * h + half) % 2
            c0 = h * EBW + half * 704
            sp_load(xs[i][:, 0:704], btab[:, c0:c0 + 704], [("xs", i)], f"xs{i}")
            S.op("act", (lambda e, i=i, h=h, half=half: e.activation(
                out=ebt[:, h, half * 704:(half + 1) * 704], in_=xs[i][:, 0:704], func=AF.Copy, scale=8.0)),
                reads=[("xs", i)], writes=[("ebt", h, half)])
    EBT_R = [("ebt", h, half) for h in range(NH) for half in range(2)]

    def rstd_from(ss_ap, out_ap, reads, wkey):
        S.op("pool", (lambda e: e.tensor_scalar(out_ap, ss_ap, 1.0 / D, EPS, ALU.mult, ALU.add)),
             reads=reads, writes=[wkey])
        S.op("pool", (lambda e: e.tensor_tensor(out_ap, out_ap, negh[:, 0:1], ALU.pow)),
             reads=[wkey, ("negh",)], writes=[wkey])

    jk_i = [0]

    junkA = pa[:, :].bitcast(BF16)[:, 0:D]
    junkB = pb[:, :].bitcast(BF16)[:, 0:D]

    def junk_buf():
        jk_i[0] ^= 1
        return (junkA, ("pa",)) if jk_i[0] else (junkB, ("pb",))

    def norm_front(src_ap, src_reads, gidx, hbt, hbkey):
        c = stat_col()
        jb, jkey = junk_buf()
        S.op("act", (lambda e: e.activation(out=jb[:, :], in_=src_ap, func=AF.Square,
                                            accum_out=stats[:, c:c + 1])),
             reads=src_reads, writes=[("st", c), jkey])
        c2 = stat_col()
        rstd_from(stats[:, c:c + 1], stats[:, c2:c2 + 1], [("st", c)], ("st", c2))
        S.op("dve", (lambda e: e.scalar_tensor_tensor(hbt[:, :], src_ap, stats[:, c2:c2 + 1],
                                                       gtab[:, gidx, :], ALU.mult, ALU.mult)),
             reads=list(src_reads) + [("st", c2), ("gtab",)], writes=[hbkey])

    def norm_back(dstT, dst_keys, tcol0, hbt, hbkey, en=None):
        bank = mm_bank()
        psb = ps[bank][:, :].bitcast(BF16).rearrange("p (k t) -> p k t", k=8)
        S.op("pe", [(lambda e, k=k: e.transpose(psb[:, k, :], hbt[:, k * P:(k + 1) * P], ident_sb[:, :]))
                    for k in range(8)],
             reads=[hbkey, ("ident",)], writes=[("ps", bank)])
        en = en or evac_engine()
        S.op(en, copy_fn(en, dstT[:, :, tcol0:tcol0 + P], psb),
             reads=[("ps", bank)], writes=dst_keys)

    def norm_to_T(src_ap, src_reads, gidx, dstT, dst_key_fn, tcol0, hbt, hbkey):
        norm_front(src_ap, src_reads, gidx, hbt, hbkey)
        norm_back(dstT, dst_key_fn(), tcol0, hbt, hbkey)

    xs_i = [0]

    p1_slot = {}

    def phase1_front(bb, tt):
        rr = 512 * bb
        i = xs_i[0] = (xs_i[0] + 1) % 2
        p1_slot[(bb, tt)] = i
        sp_load(xs[i][:, :], xin[rr + tt * P:rr + (tt + 1) * P, :], [("xs", i)], f"xs{i}")
        norm_front(xs[i][:, :], [("xs", i)], 0, hb[i], ("hb", i))

    def phase1_back(bb, tt):
        i = p1_slot[(bb, tt)]
        norm_back(hT, [("hT", tt)], tt * P, hb[i], ("hb", i))

    def phase1_tile(bb, tt):
        phase1_front(bb, tt)
        phase1_back(bb, tt)

    for tt in range(8):
        phase1_tile(0, tt)

    for b in range(nblk):
        r0 = 512 * b
        hT_all = [("hT", t) for t in range(8)]
        hT_half = [[("hT", t) for t in range(4)], [("hT", t) for t in range(4, 8)]]
        hT_mid = [("hT", t) for t in range(2, 6)]

        first_A = [True]

        def tokA(write):
            return ([], [("tokA",)]) if write else ([("tokA",)], [])

        first_B = [True]

        def fm_proj(slot_t, col0, rhs_lo, nN, hreads):
            bank = mm_bank()
            wv = slot_t[:, :].rearrange("p (k n) -> p k n", k=8)
            return bank, [(lambda e, k=k: e.matmul(ps[bank][:, :], wv[:, k, col0:col0 + P],
                                                   hT[:, k, rhs_lo:rhs_lo + 512],
                                                   start=(k == 0), stop=(k == 7))) for k in range(8)]

        slot, wt = w_acquire()
        for c in range(4):
            for hh in range(2):
                bank, fns = fm_proj(wt, c * P, hh * 512, 512, None)
                S.op("pe", fns, reads=hT_half[hh] + [("wr", slot)], writes=[("ps", bank)])
                en = evac_engine()
                rd, wr = tokA(first_A[0])
                first_A[0] = False
                S.op(en, copy_fn(en, UT[:, c, hh * 512:(hh + 1) * 512], ps[bank][:, :]),
                     reads=[("ps", bank)] + rd, writes=[("UT", c, hh)] + wr)
        for g in range(4):
            w = (2, 4, 8, 16)[g]
            ukeys = [("UT", g, 0), ("UT", g, 1)]
            S.op("dve", (lambda e, g=g, b=b: e.tensor_scalar(UT[:, g, 248:256], UT[:, g, 248:256],
                                                            hmask_sb[:, 2 * b:2 * b + 1], None, ALU.mult)),
                 reads=ukeys + [("hmask",), ("tokA",)], writes=[("UT", g, 0)])
            S.op("dve", (lambda e, g=g, b=b: e.tensor_scalar(UT[:, g, 768:776], UT[:, g, 768:776],
                                                            hmask_sb[:, 2 * b + 1:2 * b + 2], None, ALU.mult)),
                 reads=ukeys + [("hmask",), ("tokA",)], writes=[("UT", g, 1)])
            S.op("dve", (lambda e, g=g: e.tensor_tensor(pa[:, 0:527], UT[:, g, 248:775], UT[:, g, 249:776], ALU.add)),
                 reads=ukeys + [("tokA",)], writes=[("pa",)])
            cur, curk, off = pa, ("pa",), 7
            if g >= 1:
                S.op("dve", (lambda e: e.tensor_tensor(pb[:, 0:525], pa[:, 0:525], pa[:, 2:527], ALU.add)),
                     reads=[("pa",)], writes=[("pb",)])
                cur, curk, off = pb, ("pb",), 6
            if g >= 2:
                S.op("dve", (lambda e: e.tensor_tensor(pa[:, 0:521], pb[:, 0:521], pb[:, 4:525], ALU.add)),
                     reads=[("pb",)], writes=[("pa",)])
                cur, curk, off = pa, ("pa",), 4
            if g >= 3:
                S.op("dve", (lambda e: e.tensor_tensor(pb[:, 0:513], pa[:, 0:513], pa[:, 8:521], ALU.add)),
                     reads=[("pa",)], writes=[("pb",)])
                cur, curk, off = pb, ("pb",), 0
            S.op("dve", (lambda e, cur=cur, off=off, w=w: e.tensor_scalar(
                t1[:, :], cur[:, off:off + 512], 1.0 / w, None, ALU.mult)),
                reads=[curk], writes=[("t1",)])
            e0 = b * 64 + g * 16
            S.op("dve", (lambda e, cur=cur, off=off, e0=e0: e.tensor_tensor(
                t1[:, 0:8], cur[:, off:off + 8], edge_sb[:, e0:e0 + 8], ALU.mult)),
                reads=[curk, ("edge",), ("t1",)], writes=[("t1",)])
            S.op("dve", (lambda e, cur=cur, off=off, e0=e0: e.tensor_tensor(
                t1[:, 504:512], cur[:, off + 504:off + 512], edge_sb[:, e0 + 8:e0 + 16], ALU.mult)),
                reads=[curk, ("edge",), ("t1",)], writes=[("t1",)])
            S.op("dve", (lambda e, g=g: e.tensor_tensor(pooled[:, g, :], t1[:, :], UT[:, g, 256:768], ALU.subtract)),
                 reads=[("t1",)] + ukeys + [("tokA",)], writes=[("pooled", g)])

        slot, wt = w_acquire()
        for c in range(4):
            for hh in range(2):
                bank, fns = fm_proj(wt, c * P, hh * 512, 512, None)
                S.op("pe", fns, reads=hT_half[hh] + [("wr", slot)], writes=[("ps", bank)])
                en = "act"
                S.op(en, copy_fn(en, KT[:, c, hh * 512:(hh + 1) * 512], ps[bank][:, :]),
                     reads=[("ps", bank), ("tokA",)], writes=[("KT", c, hh)])
        for cb in range(2):
            slot, wt = w_acquire()
            wv = wt[:, :].rearrange("p (k n) -> p k n", k=8)
            for tt in range(8):
                bank = mm_bank()
                S.op("pe", [(lambda e, k=k, tt=tt, bank=bank, wv=wv: e.matmul(
                    ps[bank][:, :], hT[:, k, tt * P:(tt + 1) * P], wv[:, k, :],
                    start=(k == 0), stop=(k == 7))) for k in range(8)],
                    reads=[("hT", tt), ("wr", slot)], writes=[("ps", bank)])
                en = evac_engine()
                if first_B[0]:
                    rd, wr = [], [("tokB",)]
                    first_B[0] = False
                else:
                    rd, wr = [("tokB",)], []
                S.op(en, copy_fn(en, V[:, tt, cb * 512:(cb + 1) * 512], ps[bank][:, :]),
                     reads=[("ps", bank)] + rd, writes=[("V", tt, cb)] + wr)
        slot, wt = w_acquire()
        for c in range(4):
            bank, fns = fm_proj(wt, c * P, 256, 512, None)
            S.op("pe", fns, reads=hT_mid + [("wr", slot)], writes=[("ps", bank)])
            for hp in range(2):
                en = evac_engine()
                S.op(en, copy_fn(en, QT[hp * 64:(hp + 1) * 64, 2 * c + hp, :], ps[bank][hp * 64:(hp + 1) * 64, :]),
                     reads=[("ps", bank)], writes=[("QT", 2 * c + hp)])
        qi = b % 2
        S.op("pool", (lambda e, qi=qi, b=b: e.dma_start(out=qmask_sb[qi][:, :], in_=qmask[b])),
             reads=[], writes=[("qmask", qi)], dma_key=f"qm{qi}")

        if debug and b == 0:
            def dump(name, src, reads):
                S.op("sp", (lambda e: e.dma_start(out=dbg[name], in_=src)), reads=reads, writes=[("dbg", name)],
                     dma_key="dbg")
            dump("d_hT", hT[:, :, :].rearrange("p k t -> p (k t)"), hT_all)
            dump("d_KT", arenaA[:, 8192:12288], [("KT", c, hh) for c in range(4) for hh in range(2)])
            dump("d_UT", arenaA[:, 0:8192].bitcast(F32), [("UT", c, hh) for c in range(4) for hh in range(2)] +
                 [("pooled", g) for g in range(4)])
            dump("d_V", arenaB[:, :], [("V", t, cb) for t in range(8) for cb in range(2)])
            pass
            dump("d_pooled", pooled[:, :, :].rearrange("p c t -> p (c t)"), [("pooled", g) for g in range(4)])

        e_i = [0]
        p_i = [0]
        gate_slots = {}
        head_acc = {}
        pt_of = {}
        LA = 4

        def head_pre(h):
            if h % 4 == 0:
                gate_slots["gp"] = w_acquire()
                gate_slots["ga"] = w_acquire(hold_prev=1)
            col = (h % 4) * P
            t1x = t1 if h % 2 == 0 else t1b
            t1k = ("t1", h % 2)
            gslot, gwt = gate_slots["gp"]
            bank, fns = fm_proj(gwt, col, 256, 512, None)
            S.op("pe", fns, reads=hT_mid + [("wr", gslot)], writes=[("ps", bank)])
            S.op("act", (lambda e, bank=bank: e.activation(out=tnh[2][:, :], in_=ps[bank][:, :], func=AF.Tanh, scale=0.5)),
                 reads=[("ps", bank)], writes=[("tnh", 2)])
            gslot, gwt = gate_slots["ga"]
            bank, fns = fm_proj(gwt, col, 256, 512, None)
            S.op("pe", fns, reads=hT_mid + [("wr", gslot)], writes=[("ps", bank)])
            ta = h % 2
            S.op("act", (lambda e, bank=bank, ta=ta: e.activation(out=tnh[ta][:, :], in_=ps[bank][:, :], func=AF.Tanh, scale=0.5)),
                 reads=[("ps", bank)], writes=[("tnh", ta)])
            g = h // 2
            abank = mm_bank()
            S.op("pe", (lambda e, abank=abank, g=g, h=h: e.matmul(
                ps[abank][:, :], wpool_sb[:, g, (h % 2) * P:(h % 2 + 1) * P], pooled[:, g, :], start=True, stop=True)),
                reads=[("wpool",), ("pooled", g)], writes=[("ps", abank)])
            S.op("act", (lambda e, abank=abank, h=h, t1x=t1x: e.activation(
                out=t1x[:, :], in_=ps[abank][:, :], func=AF.Copy, scale=pscale_h[:, h:h + 1])),
                reads=[("ps", abank), ("pscale_h",)], writes=[t1k])
            S.op("dve", (lambda e, t1x=t1x: e.scalar_tensor_tensor(t1x[:, :], tnh[2][:, :], 1.0, t1x[:, :], ALU.add, ALU.mult)),
                 reads=[t1k, ("tnh", 2)], writes=[t1k])

        def rec_S(h, p):
            ch = h // 2
            pl = (h % 2) * 64
            bank = mm_bank()
            c0 = (14 - 2 * p) * 64
            S.op("pe", [
                (lambda e, bank=bank, p=p, h=h, ch=ch: e.matmul(
                    ps[bank][:, :], KT[:, ch, p * P:(p + 1) * P],
                    QT[:, h, :], start=True, stop=False)),
                (lambda e, bank=bank, p=p, qi=qi: e.matmul(
                    ps[bank][:, :], kone_sb[:, p * P:(p + 1) * P],
                    qmask_sb[qi][:, :], start=False, stop=False)),
                (lambda e, bank=bank, c0=c0, h=h: e.matmul(
                    ps[bank][:, :], ident_sb[:, :], ebt[:, h, c0:c0 + 512], start=False, stop=True)),
            ], reads=[("KT", ch, p // 4), ("QT", h), ("kone",), ("qmask", qi), ("tokA",), ("ident",)] + EBT_R[2 * h:2 * h + 2],
                writes=[("ps", bank)])
            pi = p_i[0] = (p_i[0] + 1) % NPT
            S.op("act", (lambda e, bank=bank, pi=pi: e.activation(out=PT[pi][:, :], in_=ps[bank][:, :],
                                                                   func=AF.Exp, scale=0.125)),
                 reads=[("ps", bank)], writes=[("PT", pi)])
            pt_of[(h, p)] = pi

        def rec_PV(h, p):
            if p == 0:
                head_acc[h] = acc_pair()
            pvb, dnb = head_acc[h]
            pi = pt_of[(h, p)]
            S.op("pe", [
                (lambda e, pi=pi, p=p, h=h, pvb=pvb: e.matmul(ps[pvb][:, :], V[:, p, h * P:(h + 1) * P],
                                                              PT[pi][:, :], start=(p == 0), stop=(p == 7))),
                (lambda e, pi=pi, p=p, dnb=dnb: e.matmul(ps[dnb][:, :], ones_sb[:, :], PT[pi][:, :],
                                                         start=(p == 0), stop=(p == 7))),
            ], reads=[("PT", pi), ("V", p, h // 4), ("ones",), ("tokB",)],
                writes=[("ps", pvb), ("ps", dnb)])

        def head_post(h):
            pvb, dnb = head_acc[h]
            t1x = t1 if h % 2 == 0 else t1b
            t1k = ("t1", h % 2)
            ta = h % 2
            S.op("dve", (lambda e, dnb=dnb: e.reciprocal(rden[:, :], ps[dnb][:, :])),
                 reads=[("ps", dnb)], writes=[("rden",)])
            S.op("dve", (lambda e, pvb=pvb: e.scalar_tensor_tensor(t2[:, :], ps[pvb][:, :], 0.5, rden[:, :], ALU.mult, ALU.mult)),
                 reads=[("ps", pvb), ("rden",)], writes=[("t2",)])
            S.op("dve", (lambda e, ta=ta: e.scalar_tensor_tensor(t3[:, :], tnh[ta][:, :], 1.0, t2[:, :], ALU.add, ALU.mult)),
                 reads=[("t2",), ("tnh", ta)], writes=[("t3",)])
            S.op("pool", (lambda e, h=h, t1x=t1x: e.tensor_tensor(mT[:, h, :], t1x[:, :], t3[:, :], ALU.add)),
                 reads=[t1k, ("t3",)], writes=[("mT", h, t) for t in range(4)])

        tiles = [(h, p) for h in range(NH) for p in range(8)]
        for idx in range(len(tiles) + LA):
            if idx < len(tiles):
                h, p = tiles[idx]
                if p == 0:
                    head_pre(h)
                rec_S(h, p)
            j = idx - LA
            if j >= 0:
                h, p = tiles[j]
                rec_PV(h, p)
                if p == 7:
                    head_post(h)
        mT_all = [("mT", f, t) for f in range(8) for t in range(4)]
        if debug and b == 0:
            dump("d_mT", mT[:, :, :].rearrange("p c t -> p (c t)"), mT_all)

        S.op("sp", (lambda e, r0=r0: e.dma_start(
            out=X1[:, :, :], in_=xin[r0 + 256:r0 + 768, :].rearrange("(t p) f -> p t f", p=P))),
            reads=[], writes=[("tokB",)] + [("x1", t) for t in range(4)], dma_key="x1")
        wo = [w_acquire(), w_acquire(hold_prev=1)]
        p5 = {}

        def p5_A1(tb):
            banks = acc_pair()
            for cb in range(2):
                oslot, owt = wo[cb]
                wv = owt[:, :].rearrange("p (k n) -> p k n", k=8)
                S.op("pe", [(lambda e, f=f, cb=cb, tb=tb, wv=wv, bk=banks[cb]: e.matmul(
                    ps[bk][:, :], mT[:, f, tb * P:(tb + 1) * P], wv[:, f, :], start=(f == 0), stop=False))
                    for f in range(7)],
                    reads=[("mT", f, tb) for f in range(7)] + [("wr", oslot)], writes=[("ps", banks[cb])])
            for cb in range(2):
                oslot, owt = wo[cb]
                wv = owt[:, :].rearrange("p (k n) -> p k n", k=8)
                S.op("pe", (lambda e, cb=cb, tb=tb, wv=wv, bk=banks[cb]: e.matmul(
                    ps[bk][:, :], mT[:, 7, tb * P:(tb + 1) * P], wv[:, 7, :], start=False, stop=True)),
                    reads=[("mT", 7, tb), ("wr", oslot)], writes=[("ps", banks[cb])])
            ca, cbb, cs = stat_col(), stat_col(), stat_col()
            for cb, cc in ((0, ca), (1, cbb)):
                jb, jkey = junk_buf()
                S.op("act", (lambda e, cb=cb, cc=cc, banks=banks, jb=jb: e.activation(
                    out=jb[:, 0:512], in_=ps[banks[cb]][:, :], func=AF.Square, accum_out=stats[:, cc:cc + 1])),
                    reads=[("ps", banks[cb])], writes=[("st", cc), jkey])
            S.op("dve", (lambda e, ca=ca, cbb=cbb, cs=cs: e.tensor_tensor(
                stats[:, cs:cs + 1], stats[:, ca:ca + 1], stats[:, cbb:cbb + 1], ALU.add)),
                reads=[("st", ca), ("st", cbb)], writes=[("st", cs)])
            p5[tb] = (banks, cs)

        def p5_A2(tb):
            banks, cs = p5[tb]
            cr = stat_col()
            rstd_from(stats[:, cs:cs + 1], stats[:, cr:cr + 1], [("st", cs)], ("st", cr))
            for cb in range(2):
                tmp = tmpS[cb]
                S.op("dve", (lambda e, cb=cb, cr=cr, banks=banks, tmp=tmp: e.scalar_tensor_tensor(
                    tmp[:, 0:512], ps[banks[cb]][:, :], stats[:, cr:cr + 1], gtab[:, 1, cb * 512:(cb + 1) * 512],
                    ALU.mult, ALU.mult)),
                    reads=[("ps", banks[cb]), ("st", cr), ("gtab",)], writes=[("tmpS", cb)])
                S.op("dve", (lambda e, cb=cb, tb=tb, tmp=tmp: e.tensor_tensor(
                    X1[:, tb, cb * 512:(cb + 1) * 512], X1[:, tb, cb * 512:(cb + 1) * 512], tmp[:, 0:512], ALU.add)),
                    reads=[("tmpS", cb), ("x1", tb), ("tokB",)], writes=[("x1", tb)])

        def p5_F(tb):
            norm_front(X1[:, tb, :], [("x1", tb), ("tokB",)], 2, hb2[tb % 2], ("hbx", tb % 2))

        def p5_Bk(tb):
            norm_back(mT, [("mT", f, tb) for f in range(8)], tb * P, hb2[tb % 2], ("hbx", tb % 2), en="dve")

        p5_A1(0)
        p5_A1(1)
        p5_A2(0)
        p5_F(0)
        p5_A1(2)
        p5_A2(1)
        p5_Bk(0)
        p5_F(1)
        p5_A1(3)
        p5_A2(2)
        p5_Bk(1)
        p5_F(2)
        p5_A2(3)
        p5_Bk(2)
        p5_F(3)
        p5_Bk(3)
        h2_all = mT_all
        if debug and b == 0:
            dump("d_h2T", mT[:, :, :].rearrange("p c t -> p (c t)"), h2_all)
            dump("d_x1", arenaB[:, :].bitcast(F32), [("x1", t) for t in range(4)] + h2_all)

        first_act = [True]
        for u in range(11):
            slot, wt = w_acquire()
            wv = wt[:, :].rearrange("p (a k n) -> p a k n", a=2, k=8)
            for jj in range(2):
                j = 2 * u + jj
                gb = mm_bank()
                S.op("pe", [(lambda e, k=k, gb=gb, jj=jj, wv=wv: e.matmul(
                    ps[gb][:, :], wv[:, 0, k, jj * P:(jj + 1) * P], mT[:, k, :], start=(k == 0), stop=(k == 7)))
                    for k in range(8)],
                    reads=h2_all + [("wr", slot)], writes=[("ps", gb)])
                ub = mm_bank()
                S.op("pe", [(lambda e, k=k, ub=ub, jj=jj, wv=wv: e.matmul(
                    ps[ub][:, :], wv[:, 1, k, jj * P:(jj + 1) * P], mT[:, k, :], start=(k == 0), stop=(k == 7)))
                    for k in range(8)],
                    reads=h2_all + [("wr", slot)], writes=[("ps", ub)])
                ti = j % 2
                S.op("act", (lambda e, gb=gb, ti=ti: e.activation(out=tnh[ti][:, :], in_=ps[gb][:, :], func=AF.Tanh, scale=0.5)),
                     reads=[("ps", gb)], writes=[("tnh", ti)])
                tw = t1 if ti == 0 else t1b
                twk = ("t1", ti)
                S.op("dve", (lambda e, gb=gb, ti=ti, tw=tw: e.scalar_tensor_tensor(
                    tw[:, :], tnh[ti][:, :], 1.0, ps[gb][:, :], ALU.add, ALU.mult)),
                    reads=[("tnh", ti), ("ps", gb)], writes=[twk])
                if first_act[0]:
                    rd, wr = [], [("tokA",)]
                    first_act[0] = False
                else:
                    rd, wr = [("tokA",)], []
                S.op("dve", (lambda e, ub=ub, tw=tw, j=j: e.scalar_tensor_tensor(
                    ACTT[:, j, :], tw[:, :], 0.5, ps[ub][:, :], ALU.mult, ALU.mult)),
                    reads=[twk, ("ps", ub)] + rd, writes=[("actT", j)] + wr)
            if b + 1 < nblk:
                if 2 <= u <= 9:
                    phase1_back(b + 1, u - 2)
                if 1 <= u <= 8:
                    phase1_front(b + 1, u - 1)
        act_all = [("actT", j) for j in range(22)]
        for pss in range(2):
            tbs = (2 * pss, 2 * pss + 1)
            bankmap = {tbs[0]: (4, 5), tbs[1]: (6, 7)}
            for u in range(6):
                slot, wt = w_acquire()
                nj = 4 if u < 5 else 2
                wv = wt[:, :].rearrange("p (j n) -> p j n", j=4)
                for tb in tbs:
                    for cb in range(2):
                        bk = bankmap[tb][cb]
                        S.op("pe", [(lambda e, jl=jl, u=u, tb=tb, cb=cb, bk=bk, wv=wv, nj=nj: e.matmul(
                            ps[bk][:, :], ACTT[:, 4 * u + jl, tb * P:(tb + 1) * P], wv[:, jl, cb * 512:(cb + 1) * 512],
                            start=(u == 0 and jl == 0), stop=(u == 5 and jl == nj - 1))) for jl in range(nj)],
                            reads=act_all + [("wr", slot), ("tokA",)], writes=[("ps", bk)])
            for tb in tbs:
                banks = bankmap[tb]
                ca, cbb, cs = stat_col(), stat_col(), stat_col()
                for cb, cc in ((0, ca), (1, cbb)):
                    jb, jkey = junk_buf()
                    S.op("act", (lambda e, cb=cb, cc=cc, banks=banks, jb=jb: e.activation(
                        out=jb[:, 0:512], in_=ps[banks[cb]][:, :], func=AF.Square, accum_out=stats[:, cc:cc + 1])),
                        reads=[("ps", banks[cb])], writes=[("st", cc), jkey])
                S.op("dve", (lambda e, ca=ca, cbb=cbb, cs=cs: e.tensor_tensor(
                    stats[:, cs:cs + 1], stats[:, ca:ca + 1], stats[:, cbb:cbb + 1], ALU.add)),
                    reads=[("st", ca), ("st", cbb)], writes=[("st", cs)])
                cr = stat_col()
                rstd_from(stats[:, cs:cs + 1], stats[:, cr:cr + 1], [("st", cs)], ("st", cr))
                for cb in range(2):
                    tmp = tmpS[cb]
                    S.op("dve", (lambda e, cb=cb, cr=cr, banks=banks, tmp=tmp: e.scalar_tensor_tensor(
                        tmp[:, 0:512], ps[banks[cb]][:, :], stats[:, cr:cr + 1], gtab[:, 3, cb * 512:(cb + 1) * 512],
                        ALU.mult, ALU.mult)),
                        reads=[("ps", banks[cb]), ("st", cr), ("gtab",)], writes=[("tmpS", cb)])
                    S.op("dve", (lambda e, cb=cb, tb=tb, tmp=tmp: e.tensor_tensor(
                        X1[:, tb, cb * 512:(cb + 1) * 512], X1[:, tb, cb * 512:(cb + 1) * 512], tmp[:, 0:512], ALU.add)),
                        reads=[("tmpS", cb), ("x1", tb), ("tokB",)], writes=[("x1", tb)])
                S.op("sp", (lambda e, tb=tb, b=b: e.dma_start(
                    out=yout[512 * b + tb * P:512 * b + (tb + 1) * P, :], in_=X1[:, tb, :])),
                    reads=[("x1", tb), ("tokB",)], writes=[("yout", b, tb)], dma_key="y")

    S.op("sp", [], reads=[], writes=[("yout", b, tb) for b in range(nblk) for tb in range(4)] +
         ([("dbg", k) for k in dbg] if debug else []))

    keys = set()
    for o in S.ops:
        if o.dma_key is not None:
            keys.add("dma:" + o.dma_key)
    for e in Sched.ENGS:
        keys.add("eng:" + e)
    sems = {k: nc.alloc_semaphore(k.replace(":", "_")) for k in sorted(keys)}
    S.finalize(nc, sems)
    with nc.Block() as block:
        @block.sync
        def _(eng):
            S.emit("sp", eng)

        @block.scalar
        def _(eng):
            S.emit("act", eng)

        @block.vector
        def _(eng):
            S.emit("dve", eng)

        @block.gpsimd
        def _(eng):
            S.emit("pool", eng)

        @block.tensor
        def _(eng):
            S.emit("pe", eng)
    return nc, S


def host_tables(attn_rpb, nblk, core, blk0=0):
    qmask = np.zeros((nblk, P, 512), np.float32)
    qmask[:, 0:16, :] = NEG
    edge = np.zeros((nblk, 4, 16), np.float32)
    hm = np.zeros((nblk, 2), np.float32)
    for b in range(nblk):
        G0 = core * TOK_CORE + 512 * (b + blk0)
        s0 = G0 % SEQ
        r0 = s0 // 64
        for i in range(8):
            r = r0 + i
            rs = min(max(r - 4, 0), 120)
            for u in range(16):
                kr = r0 - 4 + u
                if rs <= kr < rs + 8:
                    qmask[b, u, i * 64:(i + 1) * 64] = 0.0
        for g, w in enumerate((2, 4, 8, 16)):
            for j in range(16):
                t = j if j < 8 else 504 + (j - 8)
                s = s0 + t
                lo = max(s - w // 2, 0)
                hi = min(s + w // 2, SEQ)
                edge[b, g, j] = 1.0 / float(hi - lo)
        hm[b, 0] = 0.0 if s0 == 0 else 1.0
        hm[b, 1] = 0.0 if s0 + 512 == SEQ else 1.0
    edge_r = np.ascontiguousarray(np.broadcast_to(edge.reshape(1, -1), (P, nblk * 64)))
    hm_r = np.ascontiguousarray(np.broadcast_to(hm.reshape(1, -1), (P, nblk * 2)))
    return qmask, edge_r, hm_r


def bias_table(attn_rpb):
    rpb = np.asarray(attn_rpb, np.float32).reshape(NH, 15, 31)
    e = np.arange(2)[:, None, None, None]
    j = np.arange(64)[None, :, None, None]
    mm = np.arange(MMW)[None, None, :, None]
    c = np.arange(64)[None, None, None, :]
    dr = e + 10 - mm
    cs = np.clip(c - 8, 0, 48)
    ok = (np.abs(dr) <= 7) & (j >= cs) & (j < cs + 16)
    dri = np.clip(dr + 7, 0, 14)
    dci = np.clip(j - c + 15, 0, 30)
    dri_b = np.broadcast_to(dri, (2, 64, MMW, 64))
    dci_b = np.broadcast_to(dci, (2, 64, MMW, 64))
    ok_b = np.broadcast_to(ok, (2, 64, MMW, 64))
    out = np.empty((P, NH, EBW), np.float32)
    for h in range(NH):
        vals = rpb[h][dri_b, dci_b]
        tab = np.where(ok_b, vals, np.float32(NEG)).astype(np.float32)
        out[:, h, :] = tab.reshape(P, EBW)
    return out.reshape(P, NH * EBW)


def make_in_maps(inputs, nblk, blk0=0):
    xp = np.asarray(inputs["x_prompt"], np.float32).reshape(-1, D)
    xsm = np.asarray(inputs["x_sample"], np.float32).reshape(-1, D)
    xall = np.concatenate([xp, xsm], axis=0)
    ntot = xall.shape[0]
    gains = np.concatenate([np.asarray(inputs[k], np.float32).reshape(1, D) for k in
                            ("norm_mix_pre", "norm_mix_post", "norm_ffn_pre", "norm_ffn_post")], axis=1)
    gains_r = np.ascontiguousarray(np.broadcast_to(gains, (P, 4 * D)))
    pscale = np.ascontiguousarray(np.asarray(inputs["pool_scale"], np.float32).reshape(8, P).T)
    btab = bias_table(inputs["attn_rpb"])
    kone = np.zeros((P, 1024), np.float32)
    for u in range(16):
        kone[u, u * 64:(u + 1) * 64] = 1.0
    ident = np.eye(P, dtype=np.float32)
    shared = {
        "w_in": np.ascontiguousarray(np.asarray(inputs["w_in"], np.float32).reshape(D, 4608)),
        "w_gu": np.ascontiguousarray(np.asarray(inputs["w_gate_up"], np.float32).reshape(D, 2 * DFF)),
        "w_d": np.ascontiguousarray(np.asarray(inputs["w_down"], np.float32).reshape(DFF, D)),
        "w_out": np.ascontiguousarray(np.asarray(inputs["w_out"], np.float32).reshape(D, D)),
        "w_pool": np.ascontiguousarray(np.asarray(inputs["w_pool_grp"], np.float32).reshape(4, 128, 256)),
        "gains": gains_r, "pscale": pscale, "btab": btab, "kone": kone, "ident": ident,
    }
    in_maps = []
    ntok_in = 512 * nblk + 512
    for c in range(NCORE):
        lo = c * TOK_CORE + 512 * blk0 - 256
        hi = lo + ntok_in
        xin = np.zeros((ntok_in, D), np.float32)
        a, bnd = max(lo, 0), min(hi, ntot)
        xin[a - lo:bnd - lo] = xall[a:bnd]
        qm, edge_r, hm_r = host_tables(inputs["attn_rpb"], nblk, c, blk0)
        m = dict(shared)
        m.update({"xin": xin, "qmask": qm, "edge": edge_r, "hmask": hm_r})
        in_maps.append(m)
    return in_maps


_CACHE = {}


def kernel(x_prompt, x_sample, norm_mix_pre, w_in, w_pool_grp, pool_scale, attn_rpb, w_out,
           norm_mix_post, norm_ffn_pre, w_gate_up, w_down, norm_ffn_post):
    inputs = dict(x_prompt=x_prompt, x_sample=x_sample, norm_mix_pre=norm_mix_pre, w_in=w_in,
                  w_pool_grp=w_pool_grp, pool_scale=pool_scale, attn_rpb=attn_rpb, w_out=w_out,
                  norm_mix_post=norm_mix_post, norm_ffn_pre=norm_ffn_pre, w_gate_up=w_gate_up,
                  w_down=w_down, norm_ffn_post=norm_ffn_post)
    nblk = NBLK_FULL
    in_maps = make_in_maps(inputs, nblk)
    nc, _ = build(nblk)
    res = run_bass_kernel_spmd(nc, in_maps, core_ids=list(range(NCORE)))
    y = np.concatenate([np.asarray(r["yout"], np.float32) for r in res.results], axis=0)
    nprompt = np.asarray(x_prompt).shape[0] * np.asarray(x_prompt).shape[1]
    y_prompt = y[:nprompt].reshape(np.asarray(x_prompt).shape).astype(np.float32)
    y_sample = y[nprompt:].reshape(np.asarray(x_sample).shape).astype(np.float32)
    return (y_prompt, y_sample)
```

```python
import numpy as np
import concourse.bass as bass
import concourse.mybir as mybir
from concourse.bass_utils import run_bass_kernel_spmd

F32 = mybir.dt.float32
BF16 = mybir.dt.bfloat16
AF = mybir.ActivationFunctionType
ALU = mybir.AluOpType

P = 128
D = 1024
DFF = 2816
NH = 8
NCORE = 8
TOK_CORE = 10240
NBLK_FULL = 20
SEQ = 8192
MMW = 22
EBW = MMW * 64
EPS = 1e-6
NEG = -30000.0
NSLOT = 4


class Op:
    __slots__ = ("eng", "fns", "deps", "dma_key", "sig", "sem", "val", "idx")


class Sched:
    ENGS = ("pe", "act", "dve", "pool", "sp")

    def __init__(self):
        self.ops = []
        self.last_writer = {}
        self.readers = {}
        self.dma_count = {}

    def op(self, eng, fns, reads=(), writes=(), dma_key=None):
        o = Op()
        o.idx = len(self.ops)
        o.eng = eng
        o.fns = fns if isinstance(fns, (list, tuple)) else [fns]
        deps = set()
        for r in reads:
            w = self.last_writer.get(r)
            if w is not None:
                deps.add(w)
        for r in writes:
            w = self.last_writer.get(r)
            if w is not None:
                deps.add(w)
            deps.update(self.readers.get(r, ()))
        for r in reads:
            self.readers.setdefault(r, []).append(o.idx)
        for r in writes:
            self.last_writer[r] = o.idx
            self.readers[r] = []
        o.deps = deps
        o.dma_key = dma_key
        o.sig = False
        o.sem = None
        o.val = 0
        self.ops.append(o)
        return o

    def finalize(self, nc, sems):
        ops = self.ops
        for o in ops:
            for d in o.deps:
                od = ops[d]
                if od.eng == "pe" and o.eng == "pe" and od.dma_key is None:
                    continue
                od.sig = True
        cnt = {e: 0 for e in self.ENGS}
        dcnt = {}
        for o in ops:
            if o.dma_key is not None:
                dcnt[o.dma_key] = dcnt.get(o.dma_key, 0) + 1
                o.sem = sems["dma:" + o.dma_key]
                o.val = 16 * dcnt[o.dma_key]
                o.sig = True
            elif o.sig:
                cnt[o.eng] += 1
                o.sem = sems["eng:" + o.eng]
                o.val = cnt[o.eng]
        for o in ops:
            if o.dma_key in ("pro0", "pro1", "pro2"):
                o.val = 16 * dcnt[o.dma_key]
        self.max_counts = dict(cnt)
        self.dma_counts = dcnt

    def emit(self, eng_name, eng):
        ops = self.ops
        waited = {}
        for o in ops:
            if o.eng != eng_name:
                continue
            need = {}
            for d in o.deps:
                od = ops[d]
                if od.eng == "pe" and o.eng == "pe" and od.dma_key is None:
                    continue
                k = id(od.sem)
                if k not in need or need[k][1] < od.val:
                    need[k] = (od.sem, od.val)
            for k, (sem, val) in need.items():
                if waited.get(k, 0) >= val:
                    continue
                eng.wait_ge(sem, val)
                waited[k] = val
            last = None
            for fn in o.fns:
                last = fn(eng)
            if o.sig and last is not None:
                last.then_inc(o.sem, 16 if o.dma_key is not None else 1)


def build(nblk, debug=False):
    nc = bass.Bass("TRN2", target_bir_lowering=False)
    S = Sched()
    ntok_in = 512 * nblk + 512

    def din(name, shape, dt=F32):
        return nc.dram_tensor(name, list(shape), dt, kind="ExternalInput").ap()

    xin = din("xin", [ntok_in, D])
    w_in = din("w_in", [D, 4608])
    w_gu = din("w_gu", [D, 2 * DFF])
    w_d = din("w_d", [DFF, D])
    w_out = din("w_out", [D, D])
    w_pool = din("w_pool", [4, 128, 256])
    gains = din("gains", [P, 4 * D])
    pscale = din("pscale", [P, 8])
    btab = din("btab", [P, NH * EBW])
    qmask = din("qmask", [nblk, P, 512])
    kone = din("kone", [P, 1024])
    edge = din("edge", [P, nblk * 64])
    hmask = din("hmask", [P, nblk * 2])
    ident = din("ident", [P, P])
    yout = nc.dram_tensor("yout", [512 * nblk, D], F32, kind="ExternalOutput").ap()
    dbg = {}
    if debug:
        for nm, shp in (("d_hT", [P, 8 * 1024]), ("d_KT", [P, 4 * 1024]), ("d_V", [P, 8 * 1024]),
                        ("d_QT", [P, 4 * 512]), ("d_pooled", [P, 4 * 512]), ("d_mT", [P, 8 * 512]),
                        ("d_h2T", [P, 8 * 512])):
            dbg[nm] = nc.dram_tensor(nm, shp, BF16, kind="ExternalOutput").ap()
        dbg["d_x1"] = nc.dram_tensor("d_x1", [P, 4 * 1024], F32, kind="ExternalOutput").ap()
        dbg["d_UT"] = nc.dram_tensor("d_UT", [P, 4 * 1024], F32, kind="ExternalOutput").ap()

    s_win = nc.dram_tensor("s_win", [9, P, 4096], BF16).ap()
    s_wout = nc.dram_tensor("s_wout", [2, P, 4096], BF16).ap()
    s_wgu = nc.dram_tensor("s_wgu", [11, P, 4096], BF16).ap()
    s_wd = nc.dram_tensor("s_wd", [6, P, 4096], BF16).ap()

    def sb(name, shape, dt):
        return nc.alloc_sbuf_tensor(name, list(shape), dt)

    wring = [sb(f"wr{i}", [P, 4096], BF16) for i in range(NSLOT)]
    wpool_sb = sb("wpool", [P, 4, 256], BF16)
    ebt = sb("ebt", [P, NH, EBW], BF16)
    gtab = sb("gtab", [P, 4, D], F32)
    pscale_sb = sb("pscale_sb", [P, 8], F32)
    pscale_h = sb("pscale_h", [P, 8], F32)
    hmask_sb = sb("hmask_sb", [P, nblk * 2], F32)
    edge_sb = sb("edge_sb", [P, nblk * 64], F32)
    ident_sb = sb("ident_sb", [P, P], BF16)
    ones_sb = sb("ones_sb", [P, P], BF16)
    kone_sb = sb("kone_sb", [P, 1024], BF16)
    qmask_sb = [sb(f"qmask{i}", [P, 512], BF16) for i in range(2)]
    xs = [sb(f"xs{i}", [P, D], F32) for i in range(2)]
    hb = [sb(f"hb{i}", [P, D], BF16) for i in range(2)]
    hT = sb("hT", [P, 8, 1024], BF16)
    arenaA = sb("arenaA", [P, 12288], BF16)
    UT = arenaA[:, 0:8192].bitcast(F32).rearrange("p (c t) -> p c t", c=4)
    KT = arenaA[:, 8192:12288].rearrange("p (c t) -> p c t", c=4)
    ACTT = arenaA[:, 0:22 * 512].rearrange("p (j t) -> p j t", j=22)
    arenaB = sb("arenaB", [P, 8192], BF16)
    V = arenaB[:, :].rearrange("p (t f) -> p t f", t=8)
    X1 = arenaB[:, :].bitcast(F32).rearrange("p (t f) -> p t f", t=4)
    QT = sb("QTz", [P, 8, 512], BF16)
    pooled = sb("pooled", [P, 4, 512], BF16)
    mT = sb("mT", [P, 8, 512], BF16)
    NPT = 6
    PT = [sb(f"PT{i}", [P, 512], BF16) for i in range(NPT)]
    tnh = [sb(f"tnh{i}", [P, 512], F32) for i in range(3)]
    t1 = sb("t1", [P, 512], F32)
    t1b = sb("t1b", [P, 512], F32)
    tmpS = [sb(f"tmpS{i}", [P, 512], F32) for i in range(2)]
    hb2 = [sb(f"hbx{i}", [P, D], BF16) for i in range(2)]
    t2 = sb("t2", [P, 512], F32)
    t3 = sb("t3", [P, 512], F32)
    rden = sb("rden", [P, 512], F32)
    pa = sb("pa", [P, 528], F32)
    pb = sb("pb", [P, 528], F32)
    stats = sb("stats", [P, 64], F32)
    negh = sb("negh", [P, 2], F32)
    ps = [nc.alloc_psum_tensor(f"ps{i}", [P, 512], F32) for i in range(8)]

    st_i = [0]

    def stat_col():
        st_i[0] = (st_i[0] + 1) % 64
        return st_i[0]

    mm_i = [0]

    def mm_bank():
        mm_i[0] = (mm_i[0] + 1) % 4
        return mm_i[0]

    acc_i = [0]

    def acc_pair():
        acc_i[0] = (acc_i[0] + 1) % 2
        return 4 + 2 * acc_i[0], 5 + 2 * acc_i[0]

    alt = [0]

    def evac_engine():
        alt[0] ^= 1
        return "act" if alt[0] else "dve"

    def copy_fn(engname, out, in_):
        if engname == "act":
            return lambda e: e.copy(out, in_)
        return lambda e: e.tensor_copy(out, in_)

    units = []
    for _b in range(nblk):
        for u in (0, 2, 3, 4, 1, 5, 7, 6, 8):
            units.append((s_win[u], ("scr", "win", u)))
        for u in range(2):
            units.append((s_wout[u], ("scr", "wout", u)))
        for u in range(11):
            units.append((s_wgu[u], ("scr", "wgu", u)))
        for _pss in range(2):
            for u in range(6):
                units.append((s_wd[u], ("scr", "wd", u)))
    wstate = {"issued": 0, "next": 0, "released": 0}

    def w_pump():
        while wstate["issued"] < len(units) and wstate["issued"] - NSLOT < wstate["released"]:
            j = wstate["issued"]
            slot = j % NSLOT
            src, key = units[j]
            ncol = 2048 if key == ("scr", "wd", 5) else 4096
            S.op("sp", (lambda e, slot=slot, src=src, ncol=ncol: e.dma_start(out=wring[slot][:, 0:ncol], in_=src[:, 0:ncol])),
                 reads=[key], writes=[("wr", slot)], dma_key=f"wr{slot}")
            wstate["issued"] += 1

    def w_acquire(hold_prev=0):
        j = wstate["next"]
        wstate["next"] += 1
        wstate["released"] = max(wstate["released"], j - hold_prev)
        w_pump()
        assert wstate["issued"] > j, (j, wstate)
        slot = j % NSLOT
        return slot, wring[slot]

    def pro_dma(out, in_, writes, grp="pro1"):
        S.op("pool", (lambda e: e.dma_start(out=out, in_=in_)), reads=[], writes=writes, dma_key=grp)

    pro_dma(wpool_sb[:, :, :], w_pool.rearrange("g c n -> c g n"), [("wpool",)], grp="pro2")
    pro_dma(ident_sb[:, :], ident, [("ident",)], grp="pro2")
    pro_dma(kone_sb[:, :], kone, [("kone",)], grp="pro2")
    w_in_v = w_in.rearrange("(k p) n -> p k n", p=P)
    for u in (0, 2, 3, 4, 1, 5, 7, 6, 8):
        pro_dma(s_win[u].rearrange("p (k n) -> p k n", k=8), w_in_v[:, :, u * 512:(u + 1) * 512],
                [("scr", "win", u)], grp=f"pw{u}")
    w_out_v = w_out.rearrange("(k p) n -> p k n", p=P)
    for u in range(2):
        pro_dma(s_wout[u].rearrange("p (k n) -> p k n", k=8), w_out_v[:, :, u * 512:(u + 1) * 512],
                [("scr", "wout", u)])
    w_gu_v = w_gu.rearrange("(k p) n -> p k n", p=P)
    for u in range(11):
        dst = s_wgu[u].rearrange("p (a k n) -> p a k n", a=2, k=8)
        pro_dma(dst[:, 0], w_gu_v[:, :, u * 256:(u + 1) * 256], [("scr", "wgu", u, 0)])
        pro_dma(dst[:, 1], w_gu_v[:, :, DFF + u * 256:DFF + (u + 1) * 256], [("scr", "wgu", u, 1)])
    w_d_v = w_d.rearrange("(j p) n -> p j n", p=P)
    for u in range(6):
        nj = 4 if u < 5 else 2
        dst = s_wd[u].rearrange("p (j n) -> p j n", j=4)
        pro_dma(dst[:, 0:nj], w_d_v[:, 4 * u:4 * u + nj, :], [("scr", "wd", u)])
    for u in range(11):
        units_key = ("scr", "wgu", u)
        S.last_writer[units_key] = S.last_writer[("scr", "wgu", u, 1)]

    def sp_load(out, in_, writes, key):
        S.op("sp", (lambda e: e.dma_start(out=out, in_=in_)), reads=[], writes=writes, dma_key=key)

    sp_load(gtab[:, :, :], gains.rearrange("p (a n) -> p a n", a=4), [("gtab",)], "c0")
    sp_load(pscale_sb[:, :], pscale, [("pscale",)], "c1")
    sp_load(hmask_sb[:, :], hmask, [("hmask",)], "c2")
    sp_load(edge_sb[:, :], edge, [("edge",)], "c3")
    S.op("dve", (lambda e: e.memset(ones_sb[:, :], 1.0)), writes=[("ones",)])
    S.op("pool", (lambda e: e.memset(QT[:, :, :], 0.0)), writes=[("QT", h) for h in range(NH)])
    S.op("dve", (lambda e: e.memset(negh[:, :], -0.5)), writes=[("negh",)])
    S.op("dve", (lambda e: e.tensor_scalar(pscale_h[:, :], pscale_sb[:, :], 0.5, None, ALU.mult)),
         reads=[("pscale",)], writes=[("pscale_h",)])
    for h in range(NH):
        for half in range(2):
            i = (2 * h + half) % 2
            c0 = h * EBW + half * 704
            sp_load(xs[i][:, 0:704], btab[:, c0:c0 + 704], [("xs", i)], f"xs{i}")
            S.op("act", (lambda e, i=i, h=h, half=half: e.activation(
                out=ebt[:, h, half * 704:(half + 1) * 704], in_=xs[i][:, 0:704], func=AF.Copy, scale=8.0)),
                reads=[("xs", i)], writes=[("ebt", h, half)])
    EBT_R = [("ebt", h, half) for h in range(NH) for half in range(2)]

    def rstd_from(ss_ap, out_ap, reads, wkey):
        S.op("pool", (lambda e: e.tensor_scalar(out_ap, ss_ap, 1.0 / D, EPS, ALU.mult, ALU.add)),
             reads=reads, writes=[wkey])
        S.op("pool", (lambda e: e.tensor_tensor(out_ap, out_ap, negh[:, 0:1], ALU.pow)),
             reads=[wkey, ("negh",)], writes=[wkey])

    jk_i = [0]

    junkA = pa[:, :].bitcast(BF16)[:, 0:D]
    junkB = pb[:, :].bitcast(BF16)[:, 0:D]

    def junk_buf():
        jk_i[0] ^= 1
        return (junkA, ("pa",)) if jk_i[0] else (junkB, ("pb",))

    def norm_front(src_ap, src_reads, gidx, hbt, hbkey):
        c = stat_col()
        jb, jkey = junk_buf()
        S.op("act", (lambda e: e.activation(out=jb[:, :], in_=src_ap, func=AF.Square,
                                            accum_out=stats[:, c:c + 1])),
             reads=src_reads, writes=[("st", c), jkey])
        c2 = stat_col()
        rstd_from(stats[:, c:c + 1], stats[:, c2:c2 + 1], [("st", c)], ("st", c2))
        S.op("dve", (lambda e: e.scalar_tensor_tensor(hbt[:, :], src_ap, stats[:, c2:c2 + 1],
                                                       gtab[:, gidx, :], ALU.mult, ALU.mult)),
             reads=list(src_reads) + [("st", c2), ("gtab",)], writes=[hbkey])

    def norm_back(dstT, dst_keys, tcol0, hbt, hbkey, en=None):
        bank = mm_bank()
        psb = ps[bank][:, :].bitcast(BF16).rearrange("p (k t) -> p k t", k=8)
        S.op("pe", [(lambda e, k=k: e.transpose(psb[:, k, :], hbt[:, k * P:(k + 1) * P], ident_sb[:, :]))
                    for k in range(8)],
             reads=[hbkey, ("ident",)], writes=[("ps", bank)])
        en = en or evac_engine()
        S.op(en, copy_fn(en, dstT[:, :, tcol0:tcol0 + P], psb),
             reads=[("ps", bank)], writes=dst_keys)

    def norm_to_T(src_ap, src_reads, gidx, dstT, dst_key_fn, tcol0, hbt, hbkey):
        norm_front(src_ap, src_reads, gidx, hbt, hbkey)
        norm_back(dstT, dst_key_fn(), tcol0, hbt, hbkey)

    xs_i = [0]

    p1_slot = {}

    def phase1_front(bb, tt):
        rr = 512 * bb
        i = xs_i[0] = (xs_i[0] + 1) % 2
        p1_slot[(bb, tt)] = i
        sp_load(xs[i][:, :], xin[rr + tt * P:rr + (tt + 1) * P, :], [("xs", i)], f"xs{i}")
        norm_front(xs[i][:, :], [("xs", i)], 0, hb[i], ("hb", i))

    def phase1_back(bb, tt):
        i = p1_slot[(bb, tt)]
        norm_back(hT, [("hT", tt)], tt * P, hb[i], ("hb", i))

    def phase1_tile(bb, tt):
        phase1_front(bb, tt)
        phase1_back(bb, tt)

    for tt in range(8):
        phase1_tile(0, tt)

    for b in range(nblk):
        r0 = 512 * b
        hT_all = [("hT", t) for t in range(8)]
        hT_half = [[("hT", t) for t in range(4)], [("hT", t) for t in range(4, 8)]]
        hT_mid = [("hT", t) for t in range(2, 6)]

        first_A = [True]

        def tokA(write):
            return ([], [("tokA",)]) if write else ([("tokA",)], [])

        first_B = [True]

        def fm_proj(slot_t, col0, rhs_lo, nN, hreads):
            bank = mm_bank()
            wv = slot_t[:, :].rearrange("p (k n) -> p k n", k=8)
            return bank, [(lambda e, k=k: e.matmul(ps[bank][:, :], wv[:, k, col0:col0 + P],
                                                   hT[:, k, rhs_lo:rhs_lo + 512],
                                                   start=(k == 0), stop=(k == 7))) for k in range(8)]

        slot, wt = w_acquire()
        for c in range(4):
            for hh in range(2):
                bank, fns = fm_proj(wt, c * P, hh * 512, 512, None)
                S.op("pe", fns, reads=hT_half[hh] + [("wr", slot)], writes=[("ps", bank)])
                en = evac_engine()
                rd, wr = tokA(first_A[0])
                first_A[0] = False
                S.op(en, copy_fn(en, UT[:, c, hh * 512:(hh + 1) * 512], ps[bank][:, :]),
                     reads=[("ps", bank)] + rd, writes=[("UT", c, hh)] + wr)
        for g in range(4):
            w = (2, 4, 8, 16)[g]
            ukeys = [("UT", g, 0), ("UT", g, 1)]
            S.op("dve", (lambda e, g=g, b=b: e.tensor_scalar(UT[:, g, 248:256], UT[:, g, 248:256],
                                                            hmask_sb[:, 2 * b:2 * b + 1], None, ALU.mult)),
                 reads=ukeys + [("hmask",), ("tokA",)], writes=[("UT", g, 0)])
            S.op("dve", (lambda e, g=g, b=b: e.tensor_scalar(UT[:, g, 768:776], UT[:, g, 768:776],
                                                            hmask_sb[:, 2 * b + 1:2 * b + 2], None, ALU.mult)),
                 reads=ukeys + [("hmask",), ("tokA",)], writes=[("UT", g, 1)])
            S.op("dve", (lambda e, g=g: e.tensor_tensor(pa[:, 0:527], UT[:, g, 248:775], UT[:, g, 249:776], ALU.add)),
                 reads=ukeys + [("tokA",)], writes=[("pa",)])
            cur, curk, off = pa, ("pa",), 7
            if g >= 1:
                S.op("dve", (lambda e: e.tensor_tensor(pb[:, 0:525], pa[:, 0:525], pa[:, 2:527], ALU.add)),
                     reads=[("pa",)], writes=[("pb",)])
                cur, curk, off = pb, ("pb",), 6
            if g >= 2:
                S.op("dve", (lambda e: e.tensor_tensor(pa[:, 0:521], pb[:, 0:521], pb[:, 4:525], ALU.add)),
                     reads=[("pb",)], writes=[("pa",)])
                cur, curk, off = pa, ("pa",), 4
            if g >= 3:
                S.op("dve", (lambda e: e.tensor_tensor(pb[:, 0:513], pa[:, 0:513], pa[:, 8:521], ALU.add)),
                     reads=[("pa",)], writes=[("pb",)])
                cur, curk, off = pb, ("pb",), 0
            S.op("dve", (lambda e, cur=cur, off=off, w=w: e.tensor_scalar(
                t1[:, :], cur[:, off:off + 512], 1.0 / w, None, ALU.mult)),
                reads=[curk], writes=[("t1",)])
            e0 = b * 64 + g * 16
            S.op("dve", (lambda e, cur=cur, off=off, e0=e0: e.tensor_tensor(
                t1[:, 0:8], cur[:, off:off + 8], edge_sb[:, e0:e0 + 8], ALU.mult)),
                reads=[curk, ("edge",), ("t1",)], writes=[("t1",)])
            S.op("dve", (lambda e, cur=cur, off=off, e0=e0: e.tensor_tensor(
                t1[:, 504:512], cur[:, off + 504:off + 512], edge_sb[:, e0 + 8:e0 + 16], ALU.mult)),
                reads=[curk, ("edge",), ("t1",)], writes=[("t1",)])
            S.op("dve", (lambda e, g=g: e.tensor_tensor(pooled[:, g, :], t1[:, :], UT[:, g, 256:768], ALU.subtract)),
                 reads=[("t1",)] + ukeys + [("tokA",)], writes=[("pooled", g)])

        slot, wt = w_acquire()
        for c in range(4):
            for hh in range(2):
                bank, fns = fm_proj(wt, c * P, hh * 512, 512, None)
                S.op("pe", fns, reads=hT_half[hh] + [("wr", slot)], writes=[("ps", bank)])
                en = "act"
                S.op(en, copy_fn(en, KT[:, c, hh * 512:(hh + 1) * 512], ps[bank][:, :]),
                     reads=[("ps", bank), ("tokA",)], writes=[("KT", c, hh)])
        for cb in range(2):
            slot, wt = w_acquire()
            wv = wt[:, :].rearrange("p (k n) -> p k n", k=8)
            for tt in range(8):
                bank = mm_bank()
                S.op("pe", [(lambda e, k=k, tt=tt, bank=bank, wv=wv: e.matmul(
                    ps[bank][:, :], hT[:, k, tt * P:(tt + 1) * P], wv[:, k, :],
                    start=(k == 0), stop=(k == 7))) for k in range(8)],
                    reads=[("hT", tt), ("wr", slot)], writes=[("ps", bank)])
                en = evac_engine()
                if first_B[0]:
                    rd, wr = [], [("tokB",)]
                    first_B[0] = False
                else:
                    rd, wr = [("tokB",)], []
                S.op(en, copy_fn(en, V[:, tt, cb * 512:(cb + 1) * 512], ps[bank][:, :]),
                     reads=[("ps", bank)] + rd, writes=[("V", tt, cb)] + wr)
        slot, wt = w_acquire()
        for c in range(4):
            bank, fns = fm_proj(wt, c * P, 256, 512, None)
            S.op("pe", fns, reads=hT_mid + [("wr", slot)], writes=[("ps", bank)])
            for hp in range(2):
                en = evac_engine()
                S.op(en, copy_fn(en, QT[hp * 64:(hp + 1) * 64, 2 * c + hp, :], ps[bank][hp * 64:(hp + 1) * 64, :]),
                     reads=[("ps", bank)], writes=[("QT", 2 * c + hp)])
        qi = b % 2
        S.op("pool", (lambda e, qi=qi, b=b: e.dma_start(out=qmask_sb[qi][:, :], in_=qmask[b])),
             reads=[], writes=[("qmask", qi)], dma_key=f"qm{qi}")

        if debug and b == 0:
            def dump(name, src, reads):
                S.op("sp", (lambda e: e.dma_start(out=dbg[name], in_=src)), reads=reads, writes=[("dbg", name)],
                     dma_key="dbg")
            dump("d_hT", hT[:, :, :].rearrange("p k t -> p (k t)"), hT_all)
            dump("d_KT", arenaA[:, 8192:12288], [("KT", c, hh) for c in range(4) for hh in range(2)])
            dump("d_UT", arenaA[:, 0:8192].bitcast(F32), [("UT", c, hh) for c in range(4) for hh in range(2)] +
                 [("pooled", g) for g in range(4)])
            dump("d_V", arenaB[:, :], [("V", t, cb) for t in range(8) for cb in range(2)])
            pass
            dump("d_pooled", pooled[:, :, :].rearrange("p c t -> p (c t)"), [("pooled", g) for g in range(4)])

        e_i = [0]
        p_i = [0]
        gate_slots = {}
        head_acc = {}
        pt_of = {}
        LA = 4

        def head_pre(h):
            if h % 4 == 0:
                gate_slots["gp"] = w_acquire()
                gate_slots["ga"] = w_acquire(hold_prev=1)
            col = (h % 4) * P
            t1x = t1 if h % 2 == 0 else t1b
            t1k = ("t1", h % 2)
            gslot, gwt = gate_slots["gp"]
            bank, fns = fm_proj(gwt, col, 256, 512, None)
            S.op("pe", fns, reads=hT_mid + [("wr", gslot)], writes=[("ps", bank)])
            S.op("act", (lambda e, bank=bank: e.activation(out=tnh[2][:, :], in_=ps[bank][:, :], func=AF.Tanh, scale=0.5)),
                 reads=[("ps", bank)], writes=[("tnh", 2)])
            gslot, gwt = gate_slots["ga"]
            bank, fns = fm_proj(gwt, col, 256, 512, None)
            S.op("pe", fns, reads=hT_mid + [("wr", gslot)], writes=[("ps", bank)])
            ta = h % 2
            S.op("act", (lambda e, bank=bank, ta=ta: e.activation(out=tnh[ta][:, :], in_=ps[bank][:, :], func=AF.Tanh, scale=0.5)),
                 reads=[("ps", bank)], writes=[("tnh", ta)])
            g = h // 2
            abank = mm_bank()
            S.op("pe", (lambda e, abank=abank, g=g, h=h: e.matmul(
                ps[abank][:, :], wpool_sb[:, g, (h % 2) * P:(h % 2 + 1) * P], pooled[:, g, :], start=True, stop=True)),
                reads=[("wpool",), ("pooled", g)], writes=[("ps", abank)])
            S.op("act", (lambda e, abank=abank, h=h, t1x=t1x: e.activation(
                out=t1x[:, :], in_=ps[abank][:, :], func=AF.Copy, scale=pscale_h[:, h:h + 1])),
                reads=[("ps", abank), ("pscale_h",)], writes=[t1k])
            S.op("dve", (lambda e, t1x=t1x: e.scalar_tensor_tensor(t1x[:, :], tnh[2][:, :], 1.0, t1x[:, :], ALU.add, ALU.mult)),
                 reads=[t1k, ("tnh", 2)], writes=[t1k])

        def rec_S(h, p):
            ch = h // 2
            pl = (h % 2) * 64
            bank = mm_bank()
            c0 = (14 - 2 * p) * 64
            S.op("pe", [
                (lambda e, bank=bank, p=p, h=h, ch=ch: e.matmul(
                    ps[bank][:, :], KT[:, ch, p * P:(p + 1) * P],
                    QT[:, h, :], start=True, stop=False)),
                (lambda e, bank=bank, p=p, qi=qi: e.matmul(
                    ps[bank][:, :], kone_sb[:, p * P:(p + 1) * P],
                    qmask_sb[qi][:, :], start=False, stop=False)),
                (lambda e, bank=bank, c0=c0, h=h: e.matmul(
                    ps[bank][:, :], ident_sb[:, :], ebt[:, h, c0:c0 + 512], start=False, stop=True)),
            ], reads=[("KT", ch, p // 4), ("QT", h), ("kone",), ("qmask", qi), ("tokA",), ("ident",)] + EBT_R[2 * h:2 * h + 2],
                writes=[("ps", bank)])
            pi = p_i[0] = (p_i[0] + 1) % NPT
            S.op("act", (lambda e, bank=bank, pi=pi: e.activation(out=PT[pi][:, :], in_=ps[bank][:, :],
                                                                   func=AF.Exp, scale=0.125)),
                 reads=[("ps", bank)], writes=[("PT", pi)])
            pt_of[(h, p)] = pi

        def rec_PV(h, p):
            if p == 0:
                head_acc[h] = acc_pair()
            pvb, dnb = head_acc[h]
            pi = pt_of[(h, p)]
            S.op("pe", [
                (lambda e, pi=pi, p=p, h=h, pvb=pvb: e.matmul(ps[pvb][:, :], V[:, p, h * P:(h + 1) * P],
                                                              PT[pi][:, :], start=(p == 0), stop=(p == 7))),
                (lambda e, pi=pi, p=p, dnb=dnb: e.matmul(ps[dnb][:, :], ones_sb[:, :], PT[pi][:, :],
                                                         start=(p == 0), stop=(p == 7))),
            ], reads=[("PT", pi), ("V", p, h // 4), ("ones",), ("tokB",)],
                writes=[("ps", pvb), ("ps", dnb)])

        def head_post(h):
            pvb, dnb = head_acc[h]
            t1x = t1 if h % 2 == 0 else t1b
            t1k = ("t1", h % 2)
            ta = h % 2
            S.op("dve", (lambda e, dnb=dnb: e.reciprocal(rden[:, :], ps[dnb][:, :])),
                 reads=[("ps", dnb)], writes=[("rden",)])
            S.op("dve", (lambda e, pvb=pvb: e.scalar_tensor_tensor(t2[:, :], ps[pvb][:, :], 0.5, rden[:, :], ALU.mult, ALU.mult)),
                 reads=[("ps", pvb), ("rden",)], writes=[("t2",)])
            S.op("dve", (lambda e, ta=ta: e.scalar_tensor_tensor(t3[:, :], tnh[ta][:, :], 1.0, t2[:, :], ALU.add, ALU.mult)),
                 reads=[("t2",), ("tnh", ta)], writes=[("t3",)])
            S.op("pool", (lambda e, h=h, t1x=t1x: e.tensor_tensor(mT[:, h, :], t1x[:, :], t3[:, :], ALU.add)),
                 reads=[t1k, ("t3",)], writes=[("mT", h, t) for t in range(4)])

        tiles = [(h, p) for h in range(NH) for p in range(8)]
        for idx in range(len(tiles) + LA):
            if idx < len(tiles):
                h, p = tiles[idx]
                if p == 0:
                    head_pre(h)
                rec_S(h, p)
            j = idx - LA
            if j >= 0:
                h, p = tiles[j]
                rec_PV(h, p)
                if p == 7:
                    head_post(h)
        mT_all = [("mT", f, t) for f in range(8) for t in range(4)]
        if debug and b == 0:
            dump("d_mT", mT[:, :, :].rearrange("p c t -> p (c t)"), mT_all)

        S.op("sp", (lambda e, r0=r0: e.dma_start(
            out=X1[:, :, :], in_=xin[r0 + 256:r0 + 768, :].rearrange("(t p) f -> p t f", p=P))),
            reads=[], writes=[("tokB",)] + [("x1", t) for t in range(4)], dma_key="x1")
        wo = [w_acquire(), w_acquire(hold_prev=1)]
        p5 = {}

        def p5_A1(tb):
            banks = acc_pair()
            for cb in range(2):
                oslot, owt = wo[cb]
                wv = owt[:, :].rearrange("p (k n) -> p k n", k=8)
                S.op("pe", [(lambda e, f=f, cb=cb, tb=tb, wv=wv, bk=banks[cb]: e.matmul(
                    ps[bk][:, :], mT[:, f, tb * P:(tb + 1) * P], wv[:, f, :], start=(f == 0), stop=False))
                    for f in range(7)],
                    reads=[("mT", f, tb) for f in range(7)] + [("wr", oslot)], writes=[("ps", banks[cb])])
            for cb in range(2):
                oslot, owt = wo[cb]
                wv = owt[:, :].rearrange("p (k n) -> p k n", k=8)
                S.op("pe", (lambda e, cb=cb, tb=tb, wv=wv, bk=banks[cb]: e.matmul(
                    ps[bk][:, :], mT[:, 7, tb * P:(tb + 1) * P], wv[:, 7, :], start=False, stop=True)),
                    reads=[("mT", 7, tb), ("wr", oslot)], writes=[("ps", banks[cb])])
            ca, cbb, cs = stat_col(), stat_col(), stat_col()
            for cb, cc in ((0, ca), (1, cbb)):
                jb, jkey = junk_buf()
                S.op("act", (lambda e, cb=cb, cc=cc, banks=banks, jb=jb: e.activation(
                    out=jb[:, 0:512], in_=ps[banks[cb]][:, :], func=AF.Square, accum_out=stats[:, cc:cc + 1])),
                    reads=[("ps", banks[cb])], writes=[("st", cc), jkey])
            S.op("dve", (lambda e, ca=ca, cbb=cbb, cs=cs: e.tensor_tensor(
                stats[:, cs:cs + 1], stats[:, ca:ca + 1], stats[:, cbb:cbb + 1], ALU.add)),
                reads=[("st", ca), ("st", cbb)], writes=[("st", cs)])
            p5[tb] = (banks, cs)

        def p5_A2(tb):
            banks, cs = p5[tb]
            cr = stat_col()
            rstd_from(stats[:, cs:cs + 1], stats[:, cr:cr + 1], [("st", cs)], ("st", cr))
            for cb in range(2):
                tmp = tmpS[cb]
                S.op("dve", (lambda e, cb=cb, cr=cr, banks=banks, tmp=tmp: e.scalar_tensor_tensor(
                    tmp[:, 0:512], ps[banks[cb]][:, :], stats[:, cr:cr + 1], gtab[:, 1, cb * 512:(cb + 1) * 512],
                    ALU.mult, ALU.mult)),
                    reads=[("ps", banks[cb]), ("st", cr), ("gtab",)], writes=[("tmpS", cb)])
                S.op("dve", (lambda e, cb=cb, tb=tb, tmp=tmp: e.tensor_tensor(
                    X1[:, tb, cb * 512:(cb + 1) * 512], X1[:, tb, cb * 512:(cb + 1) * 512], tmp[:, 0:512], ALU.add)),
                    reads=[("tmpS", cb), ("x1", tb), ("tokB",)], writes=[("x1", tb)])

        def p5_F(tb):
            norm_front(X1[:, tb, :], [("x1", tb), ("tokB",)], 2, hb2[tb % 2], ("hbx", tb % 2))

        def p5_Bk(tb):
            norm_back(mT, [("mT", f, tb) for f in range(8)], tb * P, hb2[tb % 2], ("hbx", tb % 2), en="dve")

        p5_A1(0)
        p5_A1(1)
        p5_A2(0)
        p5_F(0)
        p5_A1(2)
        p5_A2(1)
        p5_Bk(0)
        p5_F(1)
        p5_A1(3)
        p5_A2(2)
        p5_Bk(1)
        p5_F(2)
        p5_A2(3)
        p5_Bk(2)
        p5_F(3)
        p5_Bk(3)
        h2_all = mT_all
        if debug and b == 0:
            dump("d_h2T", mT[:, :, :].rearrange("p c t -> p (c t)"), h2_all)
            dump("d_x1", arenaB[:, :].bitcast(F32), [("x1", t) for t in range(4)] + h2_all)

        first_act = [True]
        for u in range(11):
            slot, wt = w_acquire()
            wv = wt[:, :].rearrange("p (a k n) -> p a k n", a=2, k=8)
            for jj in range(2):
                j = 2 * u + jj
                gb = mm_bank()
                S.op("pe", [(lambda e, k=k, gb=gb, jj=jj, wv=wv: e.matmul(
                    ps[gb][:, :], wv[:, 0, k, jj * P:(jj + 1) * P], mT[:, k, :], start=(k == 0), stop=(k == 7)))
                    for k in range(8)],
                    reads=h2_all + [("wr", slot)], writes=[("ps", gb)])
                ub = mm_bank()
                S.op("pe", [(lambda e, k=k, ub=ub, jj=jj, wv=wv: e.matmul(
                    ps[ub][:, :], wv[:, 1, k, jj * P:(jj + 1) * P], mT[:, k, :], start=(k == 0), stop=(k == 7)))
                    for k in range(8)],
                    reads=h2_all + [("wr", slot)], writes=[("ps", ub)])
                ti = j % 2
                S.op("act", (lambda e, gb=gb, ti=ti: e.activation(out=tnh[ti][:, :], in_=ps[gb][:, :], func=AF.Tanh, scale=0.5)),
                     reads=[("ps", gb)], writes=[("tnh", ti)])
                tw = t1 if ti == 0 else t1b
                twk = ("t1", ti)
                S.op("dve", (lambda e, gb=gb, ti=ti, tw=tw: e.scalar_tensor_tensor(
                    tw[:, :], tnh[ti][:, :], 1.0, ps[gb][:, :], ALU.add, ALU.mult)),
                    reads=[("tnh", ti), ("ps", gb)], writes=[twk])
                if first_act[0]:
                    rd, wr = [], [("tokA",)]
                    first_act[0] = False
                else:
                    rd, wr = [("tokA",)], []
                S.op("dve", (lambda e, ub=ub, tw=tw, j=j: e.scalar_tensor_tensor(
                    ACTT[:, j, :], tw[:, :], 0.5, ps[ub][:, :], ALU.mult, ALU.mult)),
                    reads=[twk, ("ps", ub)] + rd, writes=[("actT", j)] + wr)
            if b + 1 < nblk:
                if 2 <= u <= 9:
                    phase1_back(b + 1, u - 2)
                if 1 <= u <= 8:
                    phase1_front(b + 1, u - 1)
        act_all = [("actT", j) for j in range(22)]
        for pss in range(2):
            tbs = (2 * pss, 2 * pss + 1)
            bankmap = {tbs[0]: (4, 5), tbs[1]: (6, 7)}
            for u in range(6):
                slot, wt = w_acquire()
                nj = 4 if u < 5 else 2
                wv = wt[:, :].rearrange("p (j n) -> p j n", j=4)
                for tb in tbs:
                    for cb in range(2):
                        bk = bankmap[tb][cb]
                        S.op("pe", [(lambda e, jl=jl, u=u, tb=tb, cb=cb, bk=bk, wv=wv, nj=nj: e.matmul(
                            ps[bk][:, :], ACTT[:, 4 * u + jl, tb * P:(tb + 1) * P], wv[:, jl, cb * 512:(cb + 1) * 512],
                            start=(u == 0 and jl == 0), stop=(u == 5 and jl == nj - 1))) for jl in range(nj)],
                            reads=act_all + [("wr", slot), ("tokA",)], writes=[("ps", bk)])
            for tb in tbs:
                banks = bankmap[tb]
                ca, cbb, cs = stat_col(), stat_col(), stat_col()
                for cb, cc in ((0, ca), (1, cbb)):
                    jb, jkey = junk_buf()
                    S.op("act", (lambda e, cb=cb, cc=cc, banks=banks, jb=jb: e.activation(
                        out=jb[:, 0:512], in_=ps[banks[cb]][:, :], func=AF.Square, accum_out=stats[:, cc:cc + 1])),
                        reads=[("ps", banks[cb])], writes=[("st", cc), jkey])
                S.op("dve", (lambda e, ca=ca, cbb=cbb, cs=cs: e.tensor_tensor(
                    stats[:, cs:cs + 1], stats[:, ca:ca + 1], stats[:, cbb:cbb + 1], ALU.add)),
                    reads=[("st", ca), ("st", cbb)], writes=[("st", cs)])
                cr = stat_col()
                rstd_from(stats[:, cs:cs + 1], stats[:, cr:cr + 1], [("st", cs)], ("st", cr))
                for cb in range(2):
                    tmp = tmpS[cb]
                    S.op("dve", (lambda e, cb=cb, cr=cr, banks=banks, tmp=tmp: e.scalar_tensor_tensor(
                        tmp[:, 0:512], ps[banks[cb]][:, :], stats[:, cr:cr + 1], gtab[:, 3, cb * 512:(cb + 1) * 512],
                        ALU.mult, ALU.mult)),
                        reads=[("ps", banks[cb]), ("st", cr), ("gtab",)], writes=[("tmpS", cb)])
                    S.op("dve", (lambda e, cb=cb, tb=tb, tmp=tmp: e.tensor_tensor(
                        X1[:, tb, cb * 512:(cb + 1) * 512], X1[:, tb, cb * 512:(cb + 1) * 512], tmp[:, 0:512], ALU.add)),
                        reads=[("tmpS", cb), ("x1", tb), ("tokB",)], writes=[("x1", tb)])
                S.op("sp", (lambda e, tb=tb, b=b: e.dma_start(
                    out=yout[512 * b + tb * P:512 * b + (tb + 1) * P, :], in_=X1[:, tb, :])),
                    reads=[("x1", tb), ("tokB",)], writes=[("yout", b, tb)], dma_key="y")

    S.op("sp", [], reads=[], writes=[("yout", b, tb) for b in range(nblk) for tb in range(4)] +
         ([("dbg", k) for k in dbg] if debug else []))

    keys = set()
    for o in S.ops:
        if o.dma_key is not None:
            keys.add("dma:" + o.dma_key)
    for e in Sched.ENGS:
        keys.add("eng:" + e)
    sems = {k: nc.alloc_semaphore(k.replace(":", "_")) for k in sorted(keys)}
    S.finalize(nc, sems)
    with nc.Block() as block:
        @block.sync
        def _(eng):
            S.emit("sp", eng)

        @block.scalar
        def _(eng):
            S.emit("act", eng)

        @block.vector
        def _(eng):
            S.emit("dve", eng)

        @block.gpsimd
        def _(eng):
            S.emit("pool", eng)

        @block.tensor
        def _(eng):
            S.emit("pe", eng)
    return nc, S


def host_tables(attn_rpb, nblk, core, blk0=0):
    qmask = np.zeros((nblk, P, 512), np.float32)
    qmask[:, 0:16, :] = NEG
    edge = np.zeros((nblk, 4, 16), np.float32)
    hm = np.zeros((nblk, 2), np.float32)
    for b in range(nblk):
        G0 = core * TOK_CORE + 512 * (b + blk0)
        s0 = G0 % SEQ
        r0 = s0 // 64
        for i in range(8):
            r = r0 + i
            rs = min(max(r - 4, 0), 120)
            for u in range(16):
                kr = r0 - 4 + u
                if rs <= kr < rs + 8:
                    qmask[b, u, i * 64:(i + 1) * 64] = 0.0
        for g, w in enumerate((2, 4, 8, 16)):
            for j in range(16):
                t = j if j < 8 else 504 + (j - 8)
                s = s0 + t
                lo = max(s - w // 2, 0)
                hi = min(s + w // 2, SEQ)
                edge[b, g, j] = 1.0 / float(hi - lo)
        hm[b, 0] = 0.0 if s0 == 0 else 1.0
        hm[b, 1] = 0.0 if s0 + 512 == SEQ else 1.0
    edge_r = np.ascontiguousarray(np.broadcast_to(edge.reshape(1, -1), (P, nblk * 64)))
    hm_r = np.ascontiguousarray(np.broadcast_to(hm.reshape(1, -1), (P, nblk * 2)))
    return qmask, edge_r, hm_r


def bias_table(attn_rpb):
    rpb = np.asarray(attn_rpb, np.float32).reshape(NH, 15, 31)
    e = np.arange(2)[:, None, None, None]
    j = np.arange(64)[None, :, None, None]
    mm = np.arange(MMW)[None, None, :, None]
    c = np.arange(64)[None, None, None, :]
    dr = e + 10 - mm
    cs = np.clip(c - 8, 0, 48)
    ok = (np.abs(dr) <= 7) & (j >= cs) & (j < cs + 16)
    dri = np.clip(dr + 7, 0, 14)
    dci = np.clip(j - c + 15, 0, 30)
    dri_b = np.broadcast_to(dri, (2, 64, MMW, 64))
    dci_b = np.broadcast_to(dci, (2, 64, MMW, 64))
    ok_b = np.broadcast_to(ok, (2, 64, MMW, 64))
    out = np.empty((P, NH, EBW), np.float32)
    for h in range(NH):
        vals = rpb[h][dri_b, dci_b]
        tab = np.where(ok_b, vals, np.float32(NEG)).astype(np.float32)
        out[:, h, :] = tab.reshape(P, EBW)
    return out.reshape(P, NH * EBW)


def make_in_maps(inputs, nblk, blk0=0):
    xp = np.asarray(inputs["x_prompt"], np.float32).reshape(-1, D)
    xsm = np.asarray(inputs["x_sample"], np.float32).reshape(-1, D)
    xall = np.concatenate([xp, xsm], axis=0)
    ntot = xall.shape[0]
    gains = np.concatenate([np.asarray(inputs[k], np.float32).reshape(1, D) for k in
                            ("norm_mix_pre", "norm_mix_post", "norm_ffn_pre", "norm_ffn_post")], axis=1)
    gains_r = np.ascontiguousarray(np.broadcast_to(gains, (P, 4 * D)))
    pscale = np.ascontiguousarray(np.asarray(inputs["pool_scale"], np.float32).reshape(8, P).T)
    btab = bias_table(inputs["attn_rpb"])
    kone = np.zeros((P, 1024), np.float32)
    for u in range(16):
        kone[u, u * 64:(u + 1) * 64] = 1.0
    ident = np.eye(P, dtype=np.float32)
    shared = {
        "w_in": np.ascontiguousarray(np.asarray(inputs["w_in"], np.float32).reshape(D, 4608)),
        "w_gu": np.ascontiguousarray(np.asarray(inputs["w_gate_up"], np.float32).reshape(D, 2 * DFF)),
        "w_d": np.ascontiguousarray(np.asarray(inputs["w_down"], np.float32).reshape(DFF, D)),
        "w_out": np.ascontiguousarray(np.asarray(inputs["w_out"], np.float32).reshape(D, D)),
        "w_pool": np.ascontiguousarray(np.asarray(inputs["w_pool_grp"], np.float32).reshape(4, 128, 256)),
        "gains": gains_r, "pscale": pscale, "btab": btab, "kone": kone, "ident": ident,
    }
    in_maps = []
    ntok_in = 512 * nblk + 512
    for c in range(NCORE):
        lo = c * TOK_CORE + 512 * blk0 - 256
        hi = lo + ntok_in
        xin = np.zeros((ntok_in, D), np.float32)
        a, bnd = max(lo, 0), min(hi, ntot)
        xin[a - lo:bnd - lo] = xall[a:bnd]
        qm, edge_r, hm_r = host_tables(inputs["attn_rpb"], nblk, c, blk0)
        m = dict(shared)
        m.update({"xin": xin, "qmask": qm, "edge": edge_r, "hmask": hm_r})
        in_maps.append(m)
    return in_maps


_CACHE = {}


def kernel(x_prompt, x_sample, norm_mix_pre, w_in, w_pool_grp, pool_scale, attn_rpb, w_out,
           norm_mix_post, norm_ffn_pre, w_gate_up, w_down, norm_ffn_post):
    inputs = dict(x_prompt=x_prompt, x_sample=x_sample, norm_mix_pre=norm_mix_pre, w_in=w_in,
                  w_pool_grp=w_pool_grp, pool_scale=pool_scale, attn_rpb=attn_rpb, w_out=w_out,
                  norm_mix_post=norm_mix_post, norm_ffn_pre=norm_ffn_pre, w_gate_up=w_gate_up,
                  w_down=w_down, norm_ffn_post=norm_ffn_post)
    nblk = NBLK_FULL
    in_maps = make_in_maps(inputs, nblk)
    nc, _ = build(nblk)
    res = run_bass_kernel_spmd(nc, in_maps, core_ids=list(range(NCORE)))
    y = np.concatenate([np.asarray(r["yout"], np.float32) for r in res.results], axis=0)
    nprompt = np.asarray(x_prompt).shape[0] * np.asarray(x_prompt).shape[1]
    y_prompt = y[:nprompt].reshape(np.asarray(x_prompt).shape).astype(np.float32)
    y_sample = y[nprompt:].reshape(np.asarray(x_sample).shape).astype(np.float32)
    return (y_prompt, y_sample)
```
